# Optimizing a Trainium2 kernel written in Bass

```python
import math
import jax, jax.numpy as jnp
from jax import lax
import numpy as np

D_MODEL = 1024
BATCH = 4
SEQ = 4096
DEPTH = 4

N_MIXERS = 2
N_GLA_LAYERS = (DEPTH + 1) // 2
N_ATTN_LAYERS = DEPTH // 2

GRID_W = 64

GLA_HEADS = 4
GLA_KEY_DIM = D_MODEL // 2
GLA_VAL_DIM = D_MODEL
GLA_DK = GLA_KEY_DIM // GLA_HEADS
GLA_DV = GLA_VAL_DIM // GLA_HEADS
GLA_GATE_RANK = 16
GLA_GATE_NORMALIZER = 16.0
GLA_CHUNK = 64
GLA_IN_DIM = 2 * GLA_KEY_DIM + 2 * GLA_VAL_DIM + 2 * GLA_GATE_RANK

ATTN_HEAD_DIM = 128
ATTN_Q_HEADS = D_MODEL // ATTN_HEAD_DIM
ATTN_KV_HEADS = 2
ATTN_GROUP = ATTN_Q_HEADS // ATTN_KV_HEADS
ATTN_QKV_DIM = (ATTN_Q_HEADS + 2 * ATTN_KV_HEADS) * ATTN_HEAD_DIM
QUERY_BLOCK = 128
ROPE_THETA = 10000.0
ROPE_PAIRS_PER_AXIS = ATTN_HEAD_DIM // 4

D_FF = 2816
CONV_WIDTH = 3

NORM_EPS = 1e-6

kernel_name = 'hybrid_gla_gqa2drope_convffn_encoder'


def rmsnorm(x, w):
    xf = x.astype(jnp.float32)
    y = xf * lax.rsqrt(jnp.mean(xf * xf, axis=-1, keepdims=True) + NORM_EPS)
    return (y * w.astype(jnp.float32)).astype(x.dtype)


def gla_chunked(q, k, v, log_a, strict):
    B, H, S, dk = q.shape
    dv = v.shape[-1]
    n = S // GLA_CHUNK
    c = lambda t: t.reshape(B, H, n, GLA_CHUNK, t.shape[-1])
    q, k, v, log_a = c(q), c(k), c(v), c(log_a)
    b = jnp.cumsum(log_a, axis=3)
    b_end = b[:, :, :, -1:, :]
    q_dec = q * jnp.exp(b)
    k_inv = k * jnp.exp(-b)
    mask = jnp.tril(jnp.ones((GLA_CHUNK, GLA_CHUNK), dtype=bool), k=-1 if strict else 0)
    att = jnp.where(mask, jnp.einsum('bhncd,bhnsd->bhncs', q_dec, k_inv), 0.0)
    o_intra = jnp.einsum('bhncs,bhnsv->bhncv', att, v)
    kv_chunk = jnp.einsum('bhncd,bhncv->bhndv', k * jnp.exp(b_end - b), v)
    decay_chunk = jnp.exp(b_end[:, :, :, 0, :])

    def step(state, inp):
        d, kv_c = inp
        return d[..., None] * state + kv_c, state

    _, s_prev = lax.scan(step, jnp.zeros((B, H, dk, dv), jnp.float32),
                         (jnp.moveaxis(decay_chunk, 2, 0), jnp.moveaxis(kv_chunk, 2, 0)))
    s_prev = jnp.moveaxis(s_prev, 0, 2)
    o_inter = jnp.einsum('bhncd,bhndv->bhncv', q_dec, s_prev)
    return (o_intra + o_inter).reshape(B, H, S, dv)


def gla_mixer(h, w_in, w_gate_up_f, b_gate_f, w_gate_up_b, b_gate_b, norm_w, w_out):
    B, S, _ = h.shape
    f32 = jnp.float32
    proj = h @ w_in
    q, k, v, g, r = jnp.split(
        proj, [GLA_KEY_DIM, 2 * GLA_KEY_DIM, 2 * GLA_KEY_DIM + GLA_VAL_DIM,
               2 * GLA_KEY_DIM + 2 * GLA_VAL_DIM], axis=-1)
    r_f, r_b = jnp.split(r, 2, axis=-1)

    def heads(t, d):
        return t.reshape(B, S, GLA_HEADS, d).transpose(0, 2, 1, 3).astype(f32)

    def log_decay(r_dir, w_up, b_up):
        logits = (r_dir @ w_up + b_up).astype(f32)
        return heads(jax.nn.log_sigmoid(logits) / GLA_GATE_NORMALIZER, GLA_DK)

    q = heads(q, GLA_DK) * (GLA_DK ** -0.5)
    k = heads(k, GLA_DK)
    v = heads(v, GLA_DV)
    la_f = log_decay(r_f, w_gate_up_f, b_gate_f)
    la_b = log_decay(r_b, w_gate_up_b, b_gate_b)
    flip = lambda t: jnp.flip(t, axis=2)
    o_f = gla_chunked(q, k, v, la_f, strict=False)
    o_b = flip(gla_chunked(flip(q), flip(k), flip(v), flip(la_b), strict=True))
    o = (o_f + o_b).transpose(0, 2, 1, 3)
    o = rmsnorm(o, norm_w) * jax.nn.silu(g.astype(f32).reshape(B, S, GLA_HEADS, GLA_DV))
    return o.reshape(B, S, GLA_VAL_DIM).astype(h.dtype) @ w_out


def apply_rope(x, cos, sin):
    half = x.shape[-1] // 2
    x1, x2 = x[..., :half], x[..., half:]
    return jnp.concatenate([x1 * cos - x2 * sin, x1 * sin + x2 * cos], axis=-1)


def attn_mixer(h, w_qkv, q_norm, k_norm, w_out, cos, sin):
    B, S, _ = h.shape
    proj = h @ w_qkv
    q, k, v = jnp.split(proj, [ATTN_Q_HEADS * ATTN_HEAD_DIM,
                               (ATTN_Q_HEADS + ATTN_KV_HEADS) * ATTN_HEAD_DIM], axis=-1)
    q = q.reshape(B, S, ATTN_Q_HEADS, ATTN_HEAD_DIM)
    k = k.reshape(B, S, ATTN_KV_HEADS, ATTN_HEAD_DIM)
    v = v.reshape(B, S, ATTN_KV_HEADS, ATTN_HEAD_DIM)
    q = apply_rope(rmsnorm(q, q_norm).astype(jnp.float32), cos, sin).astype(h.dtype)
    k = apply_rope(rmsnorm(k, k_norm).astype(jnp.float32), cos, sin).astype(h.dtype)
    n_blk = S // QUERY_BLOCK
    qb = q.reshape(B, S, ATTN_KV_HEADS, ATTN_GROUP, ATTN_HEAD_DIM).transpose(0, 2, 3, 1, 4)
    qb = qb.reshape(B, ATTN_KV_HEADS, ATTN_GROUP, n_blk, QUERY_BLOCK, ATTN_HEAD_DIM)
    qb = qb.transpose(3, 0, 1, 2, 4, 5)
    k = k.transpose(0, 2, 1, 3)
    v = v.transpose(0, 2, 1, 3)
    scale = ATTN_HEAD_DIM ** -0.5

    def block(q_blk):
        s = jnp.einsum('bkgqd,bksd->bkgqs', q_blk, k).astype(jnp.float32) * scale
        p = jax.nn.softmax(s, axis=-1).astype(v.dtype)
        return jnp.einsum('bkgqs,bksd->bkgqd', p, v)

    o = lax.map(block, qb)
    o = o.transpose(1, 0, 4, 2, 3, 5).reshape(B, S, ATTN_Q_HEADS * ATTN_HEAD_DIM)
    return o @ w_out


def conv_ffn(h, w_up, w_conv, b_conv, w_down):
    u = h @ w_up
    u = lax.conv_general_dilated(
        u, w_conv[:, None, :], window_strides=(1,), padding=[(1, 1)],
        dimension_numbers=('NWC', 'WIO', 'NWC'), feature_group_count=u.shape[-1]) + b_conv
    val, gate = jnp.split(u, 2, axis=-1)
    return (jax.nn.silu(gate) * val) @ w_down


def setup_inputs(seed: int = 0) -> dict:
    key = jax.random.key(seed)
    ks = jax.random.split(key, 20)
    nrm = lambda k, shape, s: jax.random.normal(k, shape, jnp.float32) * s
    NG, NA = N_GLA_LAYERS, N_ATTN_LAYERS
    return {
        'x': nrm(ks[0], (BATCH, SEQ, D_MODEL), 1.0),
        'norm_mix': 1.0 + nrm(ks[1], (DEPTH, D_MODEL), 0.01),
        'norm_ffn': 1.0 + nrm(ks[2], (DEPTH, D_MODEL), 0.01),
        'gla_w_in': nrm(ks[3], (NG, D_MODEL, GLA_IN_DIM), D_MODEL ** -0.5),
        'gla_w_gate_up_f': nrm(ks[4], (NG, GLA_GATE_RANK, GLA_KEY_DIM), GLA_GATE_RANK ** -0.5),
        'gla_b_gate_f': nrm(ks[5], (NG, GLA_KEY_DIM), 0.1),
        'gla_w_gate_up_b': nrm(ks[6], (NG, GLA_GATE_RANK, GLA_KEY_DIM), GLA_GATE_RANK ** -0.5),
        'gla_b_gate_b': nrm(ks[7], (NG, GLA_KEY_DIM), 0.1),
        'gla_norm': 1.0 + nrm(ks[8], (NG, GLA_DV), 0.01),
        'gla_w_out': nrm(ks[9], (NG, GLA_VAL_DIM, D_MODEL), GLA_VAL_DIM ** -0.5),
        'attn_w_qkv': nrm(ks[10], (NA, D_MODEL, ATTN_QKV_DIM), D_MODEL ** -0.5),
        'attn_q_norm': 1.0 + nrm(ks[11], (NA, ATTN_HEAD_DIM), 0.01),
        'attn_k_norm': 1.0 + nrm(ks[12], (NA, ATTN_HEAD_DIM), 0.01),
        'attn_w_out': nrm(ks[13], (NA, ATTN_Q_HEADS * ATTN_HEAD_DIM, D_MODEL),
                          (ATTN_Q_HEADS * ATTN_HEAD_DIM) ** -0.5),
        'ffn_w_up': nrm(ks[14], (DEPTH, D_MODEL, 2 * D_FF), D_MODEL ** -0.5),
        'ffn_w_conv': nrm(ks[15], (DEPTH, CONV_WIDTH, 2 * D_FF), CONV_WIDTH ** -0.5),
        'ffn_b_conv': nrm(ks[16], (DEPTH, 2 * D_FF), 0.01),
        'ffn_w_down': nrm(ks[17], (DEPTH, D_FF, D_MODEL), D_FF ** -0.5),
    }


def reference(x, norm_mix, norm_ffn, gla_w_in, gla_w_gate_up_f, gla_b_gate_f,
              gla_w_gate_up_b, gla_b_gate_b, gla_norm, gla_w_out,
              attn_w_qkv, attn_q_norm, attn_k_norm, attn_w_out,
              ffn_w_up, ffn_w_conv, ffn_b_conv, ffn_w_down):
    S = x.shape[1]
    rows = S // GRID_W
    f32 = jnp.float32
    row_idx = jnp.repeat(jnp.arange(rows, dtype=f32), GRID_W)
    col_idx = jnp.tile(jnp.arange(GRID_W, dtype=f32), rows)
    inv_freq = ROPE_THETA ** (-jnp.arange(ROPE_PAIRS_PER_AXIS, dtype=f32) / ROPE_PAIRS_PER_AXIS)
    ang = jnp.concatenate([row_idx[:, None] * inv_freq, col_idx[:, None] * inv_freq], axis=-1)
    cos = jnp.cos(ang)[None, :, None, :]
    sin = jnp.sin(ang)[None, :, None, :]

    for i in range(DEPTH):
        h = rmsnorm(x, norm_mix[i])
        j = i // N_MIXERS
        if i % N_MIXERS == 0:
            x = x + gla_mixer(h, gla_w_in[j], gla_w_gate_up_f[j], gla_b_gate_f[j],
                              gla_w_gate_up_b[j], gla_b_gate_b[j], gla_norm[j], gla_w_out[j])
        else:
            x = x + attn_mixer(h, attn_w_qkv[j], attn_q_norm[j], attn_k_norm[j],
                               attn_w_out[j], cos, sin)
        x = x + conv_ffn(rmsnorm(x, norm_ffn[i]), ffn_w_up[i], ffn_w_conv[i],
                         ffn_b_conv[i], ffn_w_down[i])
    return x
```

```python
import contextlib
import math
import numpy as np
import concourse.bass as bass
import concourse.mybir as mybir
from concourse.bass_utils import run_bass_kernel_spmd

F32 = mybir.dt.float32
BF16 = mybir.dt.bfloat16
AF = mybir.ActivationFunctionType
ALU = mybir.AluOpType
ENGS = ("pe", "act", "dve", "pool", "sp")
EPOCH = 30000

NCORES = 8
D = 1024
NCH = 8
TC = 2048
SEQ = 4096
DFF = 2816
NFT = 44
NPAIR = 22
EPS = 1e-6


class Tok:
    __slots__ = ("name", "w", "r", "rd", "dsem", "dcount")

    def __init__(self, name=""):
        self.name = name
        self.w = None
        self.r = {}
        self.rd = []
        self.dsem = None
        self.dcount = 0


class Ins:
    __slots__ = ("eng", "fn", "deps", "dma", "sig", "signaled", "stok", "inc")

    def __init__(self, eng, fn, dma):
        self.eng = eng
        self.fn = fn
        self.dma = dma
        self.deps = None
        self.sig = None
        self.signaled = False
        self.stok = None


class Prog:
    def __init__(self, nc, es):
        self.nc = nc
        self.es = es
        self.streams = {e: [] for e in ENGS}
        self.all = []
        self.nsem = 0
        self.dmas = []
        self.log = None
        self.alloc_es = es
        self.free_sems = []
        self.phase_toks = []
        self.recent_dmas = []
        self.nname = 0

    def new_sem(self, name):
        self.nsem += 1
        return self.es.enter_context(self.nc.semaphore(f"s{self.nsem}_{name}"))

    def sbuf(self, name, shape, dt):
        self.nname += 1
        return self.alloc_es.enter_context(self.nc.sbuf_tensor(f"sb{self.nname}_" + name, list(shape), dt))

    @contextlib.contextmanager
    def phase(self):
        st = contextlib.ExitStack()
        prev = self.alloc_es
        self.alloc_es = st
        self.phase_toks = []
        try:
            yield
        finally:
            self.barrier()
            for t in self.phase_toks:
                self.free_sems.append((t.dsem, t.dcount))
                t.dsem = None
            self.phase_toks = []
            st.close()
            self.alloc_es = prev

    def barrier(self):
        deps = set(self.recent_dmas)
        for e in ENGS:
            for ins in reversed(self.streams[e]):
                if ins.fn is not None and not ins.dma:
                    deps.add(ins)
                    break
        self.recent_dmas = []
        for e in ENGS:
            b = Ins(e, None, False)
            b.deps = set(deps)
            self.streams[e].append(b)
            self.all.append(b)

    def psum(self, name, shape, dt=F32):
        return self.es.enter_context(self.nc.psum_tensor("ps_" + name, list(shape), dt))

    def op(self, eng, fn, reads=(), writes=(), dma=False, inc=16):
        ins = Ins(eng, fn, dma)
        ins.inc = inc
        deps = set()
        stok = None
        if dma:
            stok = writes[0] if writes else reads[0]
            ins.stok = stok
        for t in reads:
            if t.w is not None:
                deps.add(t.w)
        for t in writes:
            if t.w is not None:
                if dma and t.w.dma and t.w.stok is stok:
                    pass
                elif (not dma) and (not t.w.dma) and t.w.eng == eng:
                    pass
                else:
                    deps.add(t.w)
            for re_, r in t.r.items():
                if dma or re_ != eng:
                    deps.add(r)
            for r in t.rd:
                deps.add(r)
        deps.discard(ins)
        ins.deps = deps
        if fn is not None:
            for t in reads:
                if dma:
                    t.rd.append(ins)
                else:
                    t.r[eng] = ins
        for t in writes:
            t.w = ins
            t.r = {}
            t.rd = []
        if dma:
            if stok.dsem is None:
                if self.free_sems:
                    stok.dsem, stok.dcount = self.free_sems.pop()
                else:
                    stok.dsem = self.new_sem("d" + stok.name)
                    stok.dcount = 0
                if self.alloc_es is not self.es:
                    self.phase_toks.append(stok)
            stok.dcount += inc
            ins.sig = (stok.dsem, stok.dcount)
            ins.signaled = True
            self.dmas.append(ins)
            self.recent_dmas.append(ins)
        self.streams[eng].append(ins)
        self.all.append(ins)
        return ins

    def dma(self, eng, out, in_, reads=(), writes=()):
        return self.op(eng, lambda e: e.dma_start(out=out, in_=in_), reads, writes, dma=True)

    def cc(self, out, in_, reads, writes):
        groups = [[0, 1], [2, 3], [4, 5], [6, 7]]
        return self.op("pool", lambda e: e.collective_compute("AllGather", ALU.bypass, replica_groups=groups, ins=[in_], outs=[out]),
                       reads, writes, dma=True, inc=1)

    def final_wait(self):
        fin = Ins("sp", None, False)
        fin.deps = set(self.dmas)
        self.streams["sp"].append(fin)
        self.all.append(fin)

    def emit(self):
        nc = self.nc
        for ins in self.all:
            for d in ins.deps:
                if d.eng == "pe" and ins.eng == "pe":
                    continue
                d.signaled = True
        for e in ENGS:
            cnt = 0
            sem = None
            for ins in self.streams[e]:
                if ins.dma or not ins.signaled or ins.fn is None:
                    continue
                if sem is None or cnt >= EPOCH:
                    sem = self.new_sem("e" + e)
                    cnt = 0
                cnt += 1
                ins.sig = (sem, cnt)
        nwaits = {e: 0 for e in ENGS}

        def run_stream(e, eng):
            known = {}
            for ins in self.streams[e]:
                need = {}
                for d in ins.deps:
                    if d.eng == "pe" and e == "pe":
                        continue
                    if d.sig is None:
                        continue
                    s, v = d.sig
                    k = id(s)
                    if known.get(k, 0) >= v:
                        continue
                    if k not in need or need[k][1] < v:
                        need[k] = (s, v)
                for k, (s, v) in need.items():
                    eng.wait_ge(s, v)
                    known[k] = v
                    nwaits[e] += 1
                    if self.log is not None:
                        self.log.append(f"{e}: wait {s.name if hasattr(s, 'name') else s} >= {v}")
                if ins.fn is None:
                    continue
                bi = ins.fn(eng)
                if ins.signaled:
                    s, v = ins.sig
                    bi.then_inc(s, ins.inc if ins.dma else 1)
                if self.log is not None:
                    self.log.append(f"{e}: L{ins.fn.__code__.co_firstlineno} sig={(ins.sig[0].name if hasattr(ins.sig[0], 'name') else ins.sig[0], ins.sig[1]) if ins.signaled else None}")

        with nc.Block() as block:
            @block.tensor
            def _(eng):
                run_stream("pe", eng)

            @block.scalar
            def _(eng):
                run_stream("act", eng)

            @block.vector
            def _(eng):
                run_stream("dve", eng)

            @block.gpsimd
            def _(eng):
                run_stream("pool", eng)

            @block.sync
            def _(eng):
                run_stream("sp", eng)
        self.nwaits = nwaits
        self.ninstr = {e: len(self.streams[e]) for e in ENGS}


class Banks:
    def __init__(self, P, n, prefix="bk", aps=None, toks=None):
        if aps is not None:
            self.aps, self.toks, n = aps, toks, len(aps)
        else:
            self.aps = [P.psum(f"{prefix}{i}", [128, 512]) for i in range(n)]
            self.toks = [Tok(f"{prefix}{i}") for i in range(n)]
        self.i = 0
        self.n = n

    def next(self):
        i = self.i
        self.i = (self.i + 1) % self.n
        return self.aps[i], self.toks[i]


class Ring:
    def __init__(self, P, name, shape, dt, n):
        self.aps = [P.sbuf(f"{name}{i}", shape, dt) for i in range(n)]
        self.toks = [Tok(f"{name}{i}") for i in range(n)]
        self.i = 0
        self.n = n

    def next(self):
        i = self.i
        self.i = (self.i + 1) % self.n
        return self.aps[i], self.toks[i]


def rmsnorm_fm(P, banks, x_sb, xtok, nw_sb, nw_tok, ones_sb, ones_tok, hT, htok, sq_ring, rstd_ring, ntt=4):
    for tt in range(ntt):
        sl = slice(tt * 512, (tt + 1) * 512)
        sq, sqt = sq_ring.next()
        P.op("act", lambda e, sq=sq, sl=sl: e.activation(out=sq[:], in_=x_sb[:, :, sl], func=AF.Square),
             [xtok[c][tt] for c in range(NCH)], [sqt])
        bk, bkt = banks.next()
        for c in range(NCH):
            P.op("pe", lambda e, bk=bk, sq=sq, c=c: e.matmul(bk[:], lhsT=ones_sb[:], rhs=sq[:, c, :], start=(c == 0), stop=(c == NCH - 1)),
                 [ones_tok, sqt], [bkt])
        rs, rst = rstd_ring.next()
        P.op("act", lambda e, rs=rs, bk=bk: e.activation(out=rs[:], in_=bk[:], func=AF.Ln, bias=EPS, scale=1.0), [bkt], [rst])
        P.op("act", lambda e, rs=rs: e.activation(out=rs[:], in_=rs[:], func=AF.Exp, scale=-0.5), [rst], [rst])
        for c in range(NCH):
            P.op("dve", lambda e, c=c, sl=sl, rs=rs: e.scalar_tensor_tensor(
                out=hT[:, c, sl], in0=x_sb[:, c, sl], scalar=nw_sb[:, c:c + 1], in1=rs[:], op0=ALU.mult, op1=ALU.mult),
                [xtok[c][tt], nw_tok, rst], [htok[tt]])


def ffn_body(P, banks, hbanks, x_sb, xtok, xh_sb, xh_tok, dr, dbg=None):
    nc = P.nc
    nw_sb = P.sbuf("f_nw", [128, NCH], F32); nw_tok = Tok("f_nw")
    cw_sb = P.sbuf("f_cw", [128, NFT * 4], F32); cw_tok = Tok("f_cw")
    ones_sb = P.sbuf("f_ones", [128, 128], BF16); ones_tok = Tok("f_ones")
    hT = P.sbuf("f_hT", [128, NCH, TC], BF16); htok = [Tok(f"f_h{t}") for t in range(4)]
    hcat = [P.sbuf(f"f_hcat{i}", [128, NCH, 2], BF16) for i in range(2)]
    hcat_tok = [Tok(f"f_hcat{i}") for i in range(2)]
    sq_ring = Ring(P, "f_sq", [128, NCH, 512], BF16, 1)
    rstd_ring = Ring(P, "f_rstd", [128, 512], F32, 2)
    P.dma("sp", nw_sb[:], dr["nw"], writes=[nw_tok])
    P.dma("sp", cw_sb[:], dr["cw"], writes=[cw_tok])
    P.op("dve", lambda e: e.memset(ones_sb[:], 1.0 / D), [], [ones_tok])

    rmsnorm_fm(P, banks, x_sb, xtok, nw_sb, nw_tok, ones_sb, ones_tok, hT, htok, sq_ring, rstd_ring)

    sqh = P.sbuf("f_sqh", [128, NCH, 2], BF16); sqh_tok = Tok("f_sqh")
    rsh = P.sbuf("f_rsh", [128, 2], F32); rsh_tok = Tok("f_rsh")
    hext = P.sbuf("f_hext", [128, NCH, 2], BF16); hext_tok = Tok("f_hext")
    P.op("act", lambda e: e.activation(out=sqh[:], in_=xh_sb[:], func=AF.Square), [xh_tok], [sqh_tok])
    bk, bkt = banks.next()
    for c in range(NCH):
        P.op("pe", lambda e, c=c, bk=bk: e.matmul(bk[:, 0:2], lhsT=ones_sb[:], rhs=sqh[:, c, :], start=(c == 0), stop=(c == NCH - 1)),
             [ones_tok, sqh_tok], [bkt])
    P.op("act", lambda e, bk=bk: e.activation(out=rsh[:], in_=bk[:, 0:2], func=AF.Sqrt, bias=EPS, scale=1.0), [bkt], [rsh_tok])
    P.op("dve", lambda e: e.reciprocal(out=rsh[:], in_=rsh[:]), [rsh_tok], [rsh_tok])
    for c in range(NCH):
        P.op("dve", lambda e, c=c: e.scalar_tensor_tensor(out=hext[:, c, :], in0=xh_sb[:, c, :], scalar=nw_sb[:, c:c + 1], in1=rsh[:],
                                                           op0=ALU.mult, op1=ALU.mult), [xh_tok, nw_tok, rsh_tok], [hext_tok])
    P.op("dve", lambda e: e.tensor_copy(out=hcat[0][:, :, 0:1], in_=hext[:, :, 0:1]), [hext_tok], [hcat_tok[0]])
    P.op("dve", lambda e: e.tensor_copy(out=hcat[0][:, :, 1:2], in_=hT[:, :, 1024:1025]), [htok[2]], [hcat_tok[0]])
    P.op("dve", lambda e: e.tensor_copy(out=hcat[1][:, :, 0:1], in_=hT[:, :, 1023:1024]), [htok[1]], [hcat_tok[1]])
    P.op("dve", lambda e: e.tensor_copy(out=hcat[1][:, :, 1:2], in_=hext[:, :, 1:2]), [hext_tok], [hcat_tok[1]])

    if dbg is not None:
        P.dma("sp", dbg["hT"].rearrange("(c p) t -> p c t", p=128), hT[:], reads=htok)
    GSZ = 11
    ws_ring = Ring(P, "f_ws", [128, 1024], F32, 2)
    wb_ring = Ring(P, "f_wb", [128, NCH, 128], BF16, 5)
    wds_ring = Ring(P, "f_wds", [128, 1024], F32, 1)
    wdb = P.sbuf("f_wdb", [128, GSZ, 1024], BF16); wdb_tok = [Tok(f"f_wdb{j}") for j in range(GSZ)]
    act = P.sbuf("f_act", [128, GSZ, 1024], BF16); act_tok = [Tok(f"f_act{j}") for j in range(GSZ)]
    u_ring = Ring(P, "f_u", [128, 1026], F32, 2)
    c_ring = Ring(P, "f_c", [128, 1024], F32, 5)
    tmp_ring = Ring.__new__(Ring)
    sqv = sq_ring.aps[0][:].rearrange("p c t -> p (c t)").bitcast(F32)
    tmp_ring.aps = [sqv[:, 0:1024], sqv[:, 1024:2048]]
    tmp_ring.toks = [Tok("f_tmp0"), Tok("f_tmp1")]
    tmp_ring.i = 0
    tmp_ring.n = 2

    pending = [None]
    PF = 4
    tiles = [(hh_, g_, jj_, kind_) for hh_ in range(2) for g_ in range(2) for jj_ in range(GSZ) for kind_ in range(2)]
    prepped = {}

    def prep(idx):
        if idx >= len(tiles) or idx in prepped:
            return
        hh_, g_, jj_, kind_ = tiles[idx]
        ft_ = 2 * (g_ * GSZ + jj_) + kind_
        ws, wst = ws_ring.next()
        P.dma("sp", ws[:], dr["wup"][ft_], writes=[wst])
        wb, wbt = wb_ring.next()
        P.op("dve", lambda e, ws=ws, wb=wb: e.tensor_copy(out=wb[:].rearrange("p c f -> p (c f)"), in_=ws[:]), [wst], [wbt])
        prepped[idx] = (wb, wbt)

    for i_ in range(PF):
        prep(i_)
    tidx = -1
    for hh in range(2):
        t0 = hh * 1024
        for g in range(2):
            for jj in range(GSZ):
                ws, wst = wds_ring.next()
                P.dma("sp", ws[:], dr["wdn"][g * GSZ + jj], writes=[wst])
                P.op("dve", lambda e, ws=ws, jj=jj: e.tensor_copy(out=wdb[:, jj, :], in_=ws[:]), [wst], [wdb_tok[jj]])
                cgate = None
                for kind in range(2):
                    ft = 2 * (g * GSZ + jj) + kind
                    tidx += 1
                    prep(tidx + PF)
                    wb, wbt = prepped.pop(tidx)
                    b0, b0t = banks.next()
                    b1, b1t = banks.next()
                    hcol = 2 * ((hh * NFT + ft) // 2)
                    hbank, hslot_tok = hbanks.next()
                    for k in range(NCH):
                        P.op("pe", lambda e, wb=wb, k=k, b0=b0, t0=t0: e.matmul(b0[:], lhsT=wb[:, k, :], rhs=hT[:, k, t0:t0 + 512], start=(k == 0), stop=(k == NCH - 1)),
                             [wbt, htok[2 * hh]], [b0t])
                        P.op("pe", lambda e, wb=wb, k=k, b1=b1, t0=t0: e.matmul(b1[:], lhsT=wb[:, k, :], rhs=hT[:, k, t0 + 512:t0 + 1024], start=(k == 0), stop=(k == NCH - 1)),
                             [wbt, htok[2 * hh + 1]], [b1t])
                        P.op("pe", lambda e, wb=wb, k=k, hcol=hcol, hh=hh, hbank=hbank: e.matmul(hbank[:, hcol:hcol + 2], lhsT=wb[:, k, :], rhs=hcat[hh][:, k, :], start=(k == 0), stop=(k == NCH - 1)),
                             [wbt, hcat_tok[hh]], [hslot_tok])
                    u, ut = u_ring.next()
                    P.op("act", lambda e, u=u, b0=b0: e.copy(out=u[:, 1:513], in_=b0[:]), [b0t], [ut])
                    P.op("act", lambda e, u=u, b1=b1: e.copy(out=u[:, 513:1025], in_=b1[:]), [b1t], [ut])
                    P.op("act", lambda e, u=u, hcol=hcol, hbank=hbank: e.copy(out=u[:, 0:1026:1025], in_=hbank[:, hcol:hcol + 2]), [hslot_tok], [ut])
                    c, ct = c_ring.next()
                    o = ft * 4
                    P.op("act", lambda e, c=c, u=u, o=o: e.activation(out=c[:], in_=u[:, 1:1025], func=AF.Identity, scale=cw_sb[:, o + 1:o + 2], bias=cw_sb[:, o + 3:o + 4]),
                         [ut, cw_tok], [ct])
                    tm, tmt = tmp_ring.next()
                    P.op("act", lambda e, tm=tm, u=u, o=o: e.activation(out=tm[:], in_=u[:, 2:1026], func=AF.Identity, scale=cw_sb[:, o + 2:o + 3]), [ut, cw_tok], [tmt])
                    if pending[0] is not None:
                        pending[0]()
                        pending[0] = None
                    P.op("dve", lambda e, c=c, u=u, o=o: e.scalar_tensor_tensor(out=c[:], in0=u[:, 0:1024], scalar=cw_sb[:, o:o + 1], in1=c[:],
                                                                               op0=ALU.mult, op1=ALU.add), [ut, cw_tok, ct], [ct])
                    P.op("pool", lambda e, c=c, tm=tm: e.tensor_tensor(out=c[:], in0=c[:], in1=tm[:], op=ALU.add), [tmt, ct], [ct])
                    if kind == 0:
                        def fin(c=c, ct=ct):
                            P.op("act", lambda e, c=c: e.activation(out=c[:], in_=c[:], func=AF.Silu), [ct], [ct])
                        cgate = (c, ct)
                    else:
                        def fin(c=c, ct=ct, cgate=cgate, jj=jj):
                            cg, cgt = cgate
                            P.op("pool", lambda e, c=c, cg=cg, jj=jj: e.tensor_tensor(out=act[:, jj, :], in0=c[:], in1=cg[:], op=ALU.mult),
                                 [ct, cgt], [act_tok[jj]])
                    pending[0] = fin
            if pending[0] is not None:
                pending[0]()
                pending[0] = None
            for d in range(NCH):
                for t2 in range(2):
                    bk, bkt = banks.next()
                    for jj in range(GSZ):
                        P.op("pe", lambda e, bk=bk, jj=jj, d=d, t2=t2: e.matmul(bk[:], lhsT=wdb[:, jj, d * 128:(d + 1) * 128], rhs=act[:, jj, t2 * 512:(t2 + 1) * 512],
                                                                                start=(jj == 0), stop=(jj == GSZ - 1)),
                             [wdb_tok[jj], act_tok[jj]], [bkt])
                    tt = hh * 2 + t2
                    sl = slice(tt * 512, (tt + 1) * 512)
                    P.op("dve", lambda e, bk=bk, d=d, sl=sl: e.tensor_tensor(out=x_sb[:, d, sl], in0=x_sb[:, d, sl], in1=bk[:], op=ALU.add),
                         [bkt, xtok[d][tt]], [xtok[d][tt]])


def build_ffn(debug=False):
    nc = bass.Bass("TRN2", target_bir_lowering=False)
    dbg = None
    if debug:
        dbg = {"hT": nc.dram_tensor("dbg_hT", [D, TC], BF16, kind="ExternalOutput").ap(),
               "act": nc.dram_tensor("dbg_act", [128, 11 * 1024], BF16, kind="ExternalOutput").ap(),
               "u": nc.dram_tensor("dbg_u", [128, 1026], F32, kind="ExternalOutput").ap(),
               "c": nc.dram_tensor("dbg_c", [128, 1024], F32, kind="ExternalOutput").ap()}
    xT = nc.dram_tensor("xT", [D, TC], F32, kind="ExternalInput").ap()
    xh = nc.dram_tensor("xh", [D, 2], F32, kind="ExternalInput").ap()
    nw = nc.dram_tensor("nw", [128, NCH], F32, kind="ExternalInput").ap()
    wup = nc.dram_tensor("wup", [NFT, 128, 1024], F32, kind="ExternalInput").ap()
    cw = nc.dram_tensor("cw", [128, NFT * 4], F32, kind="ExternalInput").ap()
    wdn = nc.dram_tensor("wdn", [NPAIR, 128, 1024], F32, kind="ExternalInput").ap()
    yT = nc.dram_tensor("yT", [D, TC], F32, kind="ExternalOutput").ap()
    with contextlib.ExitStack() as es:
        P = Prog(nc, es)
        banks = Banks(P, 6)
        hbanks = Banks(P, 2, "hb")
        x_sb = P.sbuf("x", [128, NCH, TC], F32)
        xtok = [[Tok(f"x{c}_{t}") for t in range(4)] for c in range(NCH)]
        xh_sb = P.sbuf("xh", [128, NCH, 2], F32); xh_tok = Tok("xh")
        xv = xT.rearrange("(c p) t -> p c t", p=128)
        for tt in range(4):
            sl = slice(tt * 512, (tt + 1) * 512)
            P.dma("sp", x_sb[:, :, sl], xv[:, :, sl], writes=[xtok[c][tt] for c in range(NCH)])
        P.dma("sp", xh_sb[:], xh.rearrange("(c p) t -> p c t", p=128), writes=[xh_tok])
        dr = {"nw": nw[:, :], "cw": cw[:, :], "wup": [wup[i] for i in range(NFT)], "wdn": [wdn[i] for i in range(NPAIR)]}
        ffn_body(P, banks, hbanks, x_sb, xtok, xh_sb, xh_tok, dr, dbg)
        yv = yT.rearrange("(c p) t -> p c t", p=128)
        for tt in range(4):
            sl = slice(tt * 512, (tt + 1) * 512)
            P.dma("sp", yv[:, :, sl], x_sb[:, :, sl], reads=[xtok[c][tt] for c in range(NCH)])
        P.final_wait()
        P.emit()
        print("ffn program:", P.ninstr, "waits", P.nwaits, "sems", P.nsem)
    return nc


def fm(v):
    return np.ascontiguousarray(v.reshape(NCH, 128).T)


def ffn_weights_layout(w_up, w_conv, b_conv, w_down):
    order = []
    for j in range(NPAIR):
        order.append(NPAIR + j)
        order.append(j)
    wup = np.empty((NFT, 128, NCH, 128), np.float32)
    cw = np.empty((128, NFT, 4), np.float32)
    for i, ft in enumerate(order):
        cols = slice(ft * 128, (ft + 1) * 128)
        wup[i] = w_up[:, cols].reshape(NCH, 128, 128).transpose(1, 0, 2)
        cw[:, i, 0:3] = w_conv[:, cols].T
        cw[:, i, 3] = b_conv[cols]
    wdn = np.ascontiguousarray(w_down.reshape(NPAIR, 128, D))
    return wup.reshape(NFT, 128, 1024), np.ascontiguousarray(cw.reshape(128, NFT * 4)), wdn


def halos(xT_cores):
    out = []
    for c in range(NCORES):
        h = np.zeros((D, 2), np.float32)
        if c % 2 == 1:
            h[:, 0] = xT_cores[c - 1][:, -1]
        else:
            h[:, 1] = xT_cores[c + 1][:, 0]
        out.append(h)
    return out


_PROGS = {}


def get_prog(name, builder):
    if name not in _PROGS:
        _PROGS[name] = builder()
    return _PROGS[name]


def run_ffn(xT_cores, nw, w_up, w_conv, b_conv, w_down, debug=False):
    nc = get_prog("ffn" + str(debug), lambda: build_ffn(debug))
    wup, cw, wdn = ffn_weights_layout(w_up, w_conv, b_conv, w_down)
    hl = halos(xT_cores)
    nwl = fm(nw)
    in_maps = [{"xT": xT_cores[c], "xh": hl[c], "nw": nwl, "wup": wup, "cw": cw, "wdn": wdn} for c in range(NCORES)]
    res = run_bass_kernel_spmd(nc, in_maps, core_ids=list(range(NCORES)))
    if debug:
        return [np.asarray(r["yT"]) for r in res.results], res.results
    return [np.asarray(r["yT"]) for r in res.results]


def load_x(P, xT, x_sb, xtok):
    xv = xT.rearrange("(c p) t -> p c t", p=128)
    for tt in range(4):
        sl = slice(tt * 512, (tt + 1) * 512)
        P.dma("sp", x_sb[:, :, sl], xv[:, :, sl], writes=[xtok[c][tt] for c in range(NCH)])


def store_x(P, yT, x_sb, xtok):
    yv = yT.rearrange("(c p) t -> p c t", p=128)
    for tt in range(4):
        sl = slice(tt * 512, (tt + 1) * 512)
        P.dma("sp", yv[:, :, sl], x_sb[:, :, sl], reads=[xtok[c][tt] for c in range(NCH)])


def load_cast(P, dram_ap, dst_ap, dst_tok, stage_ring, eng="pool"):
    ws, wst = stage_ring.next()
    n = dst_ap.shape[-1] if len(dst_ap.shape) == 2 else None
    P.dma("sp", ws[:, 0:dram_ap.shape[-1]], dram_ap, writes=[wst])
    if eng == "act":
        P.op(eng, lambda e: e.copy(out=dst_ap, in_=ws[:, 0:dram_ap.shape[-1]]), [wst], [dst_tok])
    else:
        P.op(eng, lambda e: e.tensor_copy(out=dst_ap, in_=ws[:, 0:dram_ap.shape[-1]]), [wst], [dst_tok])


def rope_gen(P, cs_sb, cs_tok, msk_sb, msk_tok):
    I32 = mybir.dt.int32
    TWO_PI = 2.0 * math.pi
    rowt = P.sbuf("rowt", [128, TC], F32); rowt_tok = Tok("rowt")
    colt = P.sbuf("colt", [128, TC], F32); colt_tok = Tok("colt")
    kint = P.sbuf("kint", [128, TC], I32); kint_tok = Tok("kint")
    smi = P.sbuf("ropesmi", [128, 4], I32); smi_tok = Tok("ropesmi")
    sm = P.sbuf("ropesm", [128, 8], F32); sm_tok = Tok("ropesm")
    P.op("pool", lambda e: e.iota(rowt[:], [[1, 32], [0, 64]], base=0, channel_multiplier=0, allow_small_or_imprecise_dtypes=True), [], [rowt_tok])
    P.op("pool", lambda e: e.iota(colt[:], [[0, 32], [1, 64]], base=0, channel_multiplier=0, allow_small_or_imprecise_dtypes=True), [], [colt_tok])
    P.op("pool", lambda e: e.iota(smi[:, 0:1], [[0, 1]], base=0, channel_multiplier=1), [], [smi_tok])
    P.op("dve", lambda e: e.tensor_single_scalar(out=smi[:, 1:2], in_=smi[:, 0:1], scalar=31, op=ALU.bitwise_and), [smi_tok], [smi_tok])
    P.op("dve", lambda e: e.tensor_single_scalar(out=smi[:, 2:3], in_=smi[:, 0:1], scalar=5, op=ALU.arith_shift_right), [smi_tok], [smi_tok])
    P.op("dve", lambda e: e.tensor_single_scalar(out=smi[:, 3:4], in_=smi[:, 2:3], scalar=1, op=ALU.bitwise_and), [smi_tok], [smi_tok])
    P.op("dve", lambda e: e.tensor_copy(out=sm[:, 1:2], in_=smi[:, 1:2]), [smi_tok], [sm_tok])
    P.op("dve", lambda e: e.tensor_copy(out=sm[:, 4:5], in_=smi[:, 3:4]), [smi_tok], [sm_tok])
    P.op("act", lambda e: e.activation(out=sm[:, 2:3], in_=sm[:, 1:2], func=AF.Exp, scale=-math.log(10000.0) / 32.0), [sm_tok], [sm_tok])
    P.op("dve", lambda e: e.tensor_scalar(out=sm[:, 5:6], in0=sm[:, 4:5], scalar1=-32.0, scalar2=32.0, op0=ALU.mult, op1=ALU.add), [sm_tok], [sm_tok])
    P.op("dve", lambda e: e.tensor_tensor(out=sm[:, 5:6], in0=sm[:, 5:6], in1=msk_sb[:, 1:2], op=ALU.mult), [sm_tok, msk_tok], [sm_tok])
    P.op("dve", lambda e: e.tensor_tensor(out=sm[:, 6:7], in0=sm[:, 5:6], in1=sm[:, 2:3], op=ALU.mult), [sm_tok], [sm_tok])
    P.op("dve", lambda e: e.tensor_tensor(out=colt[:], in0=colt[:], in1=rowt[:], op=ALU.subtract), [colt_tok, rowt_tok], [colt_tok])
    P.op("dve", lambda e: e.scalar_tensor_tensor(out=rowt[:], in0=colt[:], scalar=sm[:, 4:5], in1=rowt[:], op0=ALU.mult, op1=ALU.add), [colt_tok, rowt_tok, sm_tok], [rowt_tok])
    P.op("act", lambda e: e.activation(out=rowt[:], in_=rowt[:], func=AF.Identity, scale=sm[:, 2:3], bias=sm[:, 6:7]), [rowt_tok, sm_tok], [rowt_tok])
    for shift, lo in ((0.0, TC), (math.pi / 2.0, 0)):
        P.op("dve", lambda e, shift=shift: e.tensor_scalar(out=kint[:], in0=rowt[:], scalar1=shift, scalar2=1.0 / TWO_PI, op0=ALU.add, op1=ALU.mult), [rowt_tok], [kint_tok])
        P.op("dve", lambda e: e.tensor_copy(out=colt[:], in_=kint[:]), [kint_tok], [colt_tok])
        P.op("dve", lambda e: e.scalar_tensor_tensor(out=colt[:], in0=colt[:], scalar=-TWO_PI, in1=rowt[:], op0=ALU.mult, op1=ALU.add), [colt_tok, rowt_tok], [colt_tok])
        P.op("act", lambda e, shift=shift, lo=lo: e.activation(out=cs_sb[:, lo:lo + TC], in_=colt[:], func=AF.Sin, scale=1.0 - 2e-6, bias=shift * (1.0 - 2e-6)), [colt_tok], [cs_tok])


def attn1_body(P, banks, x_sb, xtok, dr):
    nw, wqk, wv, gains, cs, rot = dr["nw"], dr["wqk"], dr["wv"], dr["gains"], dr["cs"], dr["rot"]
    if True:
        nw_sb = P.sbuf("nw", [128, NCH], F32); nw_tok = Tok("nw")
        g_sb = P.sbuf("g", [128, 2], F32); g_tok = Tok("g")
        cs_sb = P.sbuf("cs", [128, 2 * TC], F32); cs_tok = Tok("cs")
        rot32 = P.sbuf("rot32", [128, 128], F32); rot32_tok = Tok("rot32")
        rotb = P.sbuf("rotb", [128, 128], BF16); rotb_tok = Tok("rotb")
        onesD = P.sbuf("onesD", [128, 128], BF16); onesD_tok = Tok("onesD")
        onesH = P.sbuf("onesH", [128, 128], BF16); onesH_tok = Tok("onesH")
        P.dma("sp", nw_sb[:], nw, writes=[nw_tok])
        P.dma("sp", g_sb[:], gains, writes=[g_tok])
        P.dma("sp", cs_sb[:], cs, reads=([dr["cs_dtok"]] if dr.get("cs_dtok") is not None else []), writes=[cs_tok])
        P.dma("sp", rot32[:], rot, writes=[rot32_tok])
        P.op("dve", lambda e: e.tensor_copy(out=rotb[:], in_=rot32[:]), [rot32_tok], [rotb_tok])
        P.op("dve", lambda e: e.memset(onesD[:], 1.0 / D), [], [onesD_tok])
        P.op("dve", lambda e: e.memset(onesH[:], 1.0 / 128), [], [onesH_tok])
        hT = P.sbuf("hT", [128, NCH, TC], BF16); htok = [Tok(f"h{t}") for t in range(4)]
        sq_ring = Ring(P, "sq", [128, NCH, 512], BF16, 1)
        rstd_ring = Ring(P, "rstd", [128, 512], F32, 2)
        rmsnorm_fm(P, banks, x_sb, xtok, nw_sb, nw_tok, onesD, onesD_tok, hT, htok, sq_ring, rstd_ring)

        ws_ring = Ring(P, "ws", [128, 2048], F32, 2)
        wb_ring = Ring(P, "wb", [128, NCH, 128], BF16, 3)
        sqh_ring = Ring(P, "sqh", [128, 512], BF16, 3)
        rs_ring = Ring(P, "rs", [128, 512], F32, 3)
        qn_ring = Ring(P, "qn", [128, 512], F32, 4)
        qnb_ring = Ring(P, "qnb", [128, 512], BF16, 4)
        t1_ring = Ring(P, "t1", [128, 512], F32, 2)
        t2_ring = Ring(P, "t2", [128, 512], F32, 2)
        qo_ring = Ring(P, "qo", [128, TC], BF16, 3)
        wvb = P.sbuf("wvb", [128, NCH, 256], BF16); wvb_tok = Tok("wvb")
        load_cast(P, wv, wvb[:].rearrange("p c f -> p (c f)"), wvb_tok, ws_ring)
        v_sb = P.sbuf("v", [128, 16, 256], BF16); v_tok = Tok("v")
        for i in range(16):
            bk, bkt = banks.next()
            for k in range(NCH):
                P.op("pe", lambda e, bk=bk, i=i, k=k: e.matmul(bk[:, 0:256], lhsT=hT[:, k, i * 128:(i + 1) * 128], rhs=wvb[:, k, :], start=(k == 0), stop=(k == NCH - 1)),
                     [wvb_tok, htok[i // 4]], [bkt])
            P.op("act", lambda e, bk=bk, i=i: e.copy(out=v_sb[:, i, :], in_=bk[:, 0:256]), [bkt], [v_tok])
        P.dma("sp", dr["v_out"].rearrange("(i p) f -> p i f", p=128), v_sb[:], reads=[v_tok], writes=[dr["kv_dtok"]])
        forder = [8, 9, 0, 1, 2, 3, 4, 5, 6, 7]
        items = [(ft, tt) for ft in forder for tt in range(4)]
        st = {}
        wbs = {}
        qos = {}

        def stage_a(i):
            ft, tt = items[i]
            if tt == 0:
                pos_ = forder.index(ft)
                for f_ in forder[pos_:pos_ + 2]:
                    if f_ not in wbs:
                        wb, wbt = wb_ring.next()
                        load_cast(P, wqk[f_], wb[:].rearrange("p c f -> p (c f)"), wbt, ws_ring, eng="dve")
                        wbs[f_] = (wb, wbt)
                qos[ft] = qo_ring.next()
            wb, wbt = wbs[ft]
            sl = slice(tt * 512, (tt + 1) * 512)
            bk, bkt = banks.next()
            for k in range(NCH):
                P.op("pe", lambda e, bk=bk, wb=wb, k=k, sl=sl: e.matmul(bk[:], lhsT=wb[:, k, :], rhs=hT[:, k, sl], start=(k == 0), stop=(k == NCH - 1)),
                     [wbt, htok[tt]], [bkt])
            sq, sqt = sqh_ring.next()
            P.op("act", lambda e, sq=sq, bk=bk: e.activation(out=sq[:], in_=bk[:], func=AF.Square), [bkt], [sqt])
            st[i] = {"bk": bk, "bkt": bkt, "sq": sq, "sqt": sqt}

        def stage_b(i):
            ft, tt = items[i]
            d_ = st[i]
            gcol = 0 if ft < 8 else 1
            b2, b2t = banks.next()
            P.op("pe", lambda e, b2=b2, sq=d_["sq"]: e.matmul(b2[:], lhsT=onesH[:], rhs=sq[:], start=True, stop=True), [onesH_tok, d_["sqt"]], [b2t])
            rs, rst = rs_ring.next()
            P.op("act", lambda e, rs=rs, b2=b2: e.activation(out=rs[:], in_=b2[:], func=AF.Ln, bias=EPS, scale=1.0), [b2t], [rst])
            P.op("act", lambda e, rs=rs: e.activation(out=rs[:], in_=rs[:], func=AF.Exp, scale=-0.5), [rst], [rst])
            qn, qnt = qn_ring.next()
            P.op("dve", lambda e, qn=qn, bk=d_["bk"], rs=rs, gcol=gcol: e.scalar_tensor_tensor(out=qn[:], in0=bk[:], scalar=g_sb[:, gcol:gcol + 1], in1=rs[:],
                                                                                         op0=ALU.mult, op1=ALU.mult), [d_["bkt"], g_tok, rst], [qnt])
            qnb, qnbt = qnb_ring.next()
            P.op("act", lambda e, qnb=qnb, qn=qn: e.copy(out=qnb[:], in_=qn[:]), [qnt], [qnbt])
            d_.update({"qn": qn, "qnt": qnt, "qnb": qnb, "qnbt": qnbt})

        def stage_c(i):
            ft, tt = items[i]
            d_ = st.pop(i)
            sl = slice(tt * 512, (tt + 1) * 512)
            qo, qot = qos[ft]
            b3, b3t = banks.next()
            P.op("pe", lambda e, b3=b3, qnb=d_["qnb"]: e.matmul(b3[:], lhsT=rotb[:], rhs=qnb[:], start=True, stop=True), [rotb_tok, d_["qnbt"]], [b3t])
            t1, t1t = t1_ring.next()
            P.op("pool", lambda e, t1=t1, qn=d_["qn"], sl=sl: e.tensor_tensor(out=t1[:], in0=qn[:], in1=cs_sb[:, sl], op=ALU.mult), [d_["qnt"], cs_tok], [t1t])
            t2, t2t = t2_ring.next()
            P.op("dve", lambda e, t2=t2, b3=b3, tt=tt: e.tensor_tensor(out=t2[:], in0=b3[:], in1=cs_sb[:, TC + tt * 512:TC + (tt + 1) * 512], op=ALU.mult),
                 [b3t, cs_tok], [t2t])
            P.op("dve", lambda e, qo=qo, t1=t1, t2=t2, sl=sl: e.tensor_tensor(out=qo[:, sl], in0=t1[:], in1=t2[:], op=ALU.add), [t1t, t2t], [qot])
            if tt == 3:
                dst = dr["q_out"][ft] if ft < 8 else dr["k_out"][ft - 8]
                P.dma("sp", dst, qo[:], reads=[qot], writes=[dr["q_dtok"] if ft < 8 else dr["kv_dtok"]])
                if ft == 9 and dr.get("after_kv") is not None:
                    dr["after_kv"]()

        n_items = len(items)
        for i in range(n_items + 3):
            if i < n_items:
                stage_a(i)
            if 0 <= i - 1 < n_items:
                stage_b(i - 1)
            if 0 <= i - 3 < n_items:
                stage_c(i - 3)


def attn2_body(P, sbanks, obanks, dbanks, x_sb, xtok, dr):
    SCALE = 128 ** -0.5
    if True:
        q_sb = P.sbuf("q", [128, 8, TC], BF16); q_tok = [Tok(f"q{h}") for h in range(8)]
        k_sb = P.sbuf("k", [128, 2, SEQ], BF16); k_tok = [Tok(f"k{h}") for h in range(2)]
        v_sb = P.sbuf("v", [128, 32, 256], BF16); v_tok = Tok("v")
        ones_b = P.sbuf("ones", [128, 128], BF16); ones_tok = Tok("ones")
        P.op("dve", lambda e: e.memset(ones_b[:], 1.0), [], [ones_tok])
        for h in range(8):
            P.dma("sp", q_sb[:, h, :], dr["q"][h], reads=[dr["q_dtok"]], writes=[q_tok[h]])
        for h in range(2):
            for r in range(2):
                P.dma("sp", k_sb[:, h, r * TC:(r + 1) * TC], dr["k"][h][r], reads=[dr["kvg_dtok"]], writes=[k_tok[h]])
        for r in range(2):
            P.dma("sp", v_sb[:, 16 * r:16 * r + 16, :], dr["v"][r].rearrange("(i p) f -> p i f", p=128), reads=[dr["kvg_dtok"]], writes=[v_tok])
        oT = P.sbuf("oT", [128, 8, TC], BF16); o_tok = [[Tok(f"o{h}_{t}") for t in range(4)] for h in range(8)]
        p_ring = Ring(P, "p", [128, 512], BF16, 3)
        rd_ring = Ring(P, "rd", [128, 512], F32, 2)
        ws_ring = Ring(P, "ws", [128, 1024], F32, 2)
        wob = P.sbuf("wob", [128, NCH, D], BF16); wob_tok = [Tok(f"wob{c}") for c in range(NCH)]
        for c in range(NCH):
            load_cast(P, dr["wo"][c], wob[:, c, :], wob_tok[c], ws_ring, eng="dve")
        steps = [(h, qb, kt) for h in range(8) for qb in range(4) for kt in range(32)]
        sc = {}

        def issue_s(i):
            h, qb, kt = steps[i]
            bk, bkt = sbanks.next()
            P.op("pe", lambda e, bk=bk, h=h, qb=qb, kt=kt: e.matmul(bk[:], lhsT=k_sb[:, h // 4, kt * 128:(kt + 1) * 128], rhs=q_sb[:, h, qb * 512:(qb + 1) * 512],
                                                                  start=True, stop=True), [k_tok[h // 4], q_tok[h]], [bkt])
            sc[i] = (bk, bkt)

        issue_s(0)
        issue_s(1)
        ob = db = None
        for i, (h, qb, kt) in enumerate(steps):
            if i + 2 < len(steps):
                issue_s(i + 2)
            bk, bkt = sc.pop(i)
            p, pt = p_ring.next()
            P.op("act", lambda e, p=p, bk=bk: e.activation(out=p[:], in_=bk[:], func=AF.Exp, scale=SCALE), [bkt], [pt])
            if kt == 0:
                ob = obanks.next()
                db = dbanks.next()
            P.op("pe", lambda e, ob=ob, p=p, h=h, kt=kt: e.matmul(ob[0][:], lhsT=v_sb[:, kt, (h // 4) * 128:(h // 4 + 1) * 128], rhs=p[:], start=(kt == 0), stop=(kt == 31)),
                 [v_tok, pt], [ob[1]])
            P.op("pe", lambda e, db=db, p=p, kt=kt: e.matmul(db[0][:], lhsT=ones_b[:], rhs=p[:], start=(kt == 0), stop=(kt == 31)),
                 [ones_tok, pt], [db[1]])
            if kt == 31:
                rd, rdt = rd_ring.next()
                P.op("dve", lambda e, rd=rd, db=db: e.reciprocal(out=rd[:], in_=db[0][:]), [db[1]], [rdt])
                P.op("dve", lambda e, rd=rd, ob=ob, h=h, qb=qb: e.tensor_tensor(out=oT[:, h, qb * 512:(qb + 1) * 512], in0=ob[0][:], in1=rd[:], op=ALU.mult),
                     [ob[1], rdt], [o_tok[h][qb]])
        for d in range(NCH):
            for tt in range(4):
                sl = slice(tt * 512, (tt + 1) * 512)
                bk, bkt = sbanks.next()
                for c in range(NCH):
                    P.op("pe", lambda e, bk=bk, c=c, d=d, sl=sl: e.matmul(bk[:], lhsT=wob[:, c, d * 128:(d + 1) * 128], rhs=oT[:, c, sl], start=(c == 0), stop=(c == NCH - 1)),
                         [wob_tok[c], o_tok[c][tt]], [bkt])
                P.op("dve", lambda e, bk=bk, d=d, sl=sl: e.tensor_tensor(out=x_sb[:, d, sl], in0=x_sb[:, d, sl], in1=bk[:], op=ALU.add),
                     [bkt, xtok[d][tt]], [xtok[d][tt]])


def rope_tables():
    pos = np.arange(SEQ)
    row = (pos // 64).astype(np.float32)
    col = (pos % 64).astype(np.float32)
    inv_freq = (np.float32(10000.0) ** (-np.arange(32, dtype=np.float32) / np.float32(32))).astype(np.float32)
    ang = np.concatenate([row[:, None] * inv_freq, col[:, None] * inv_freq], axis=-1).astype(np.float32)
    cos = np.cos(ang).astype(np.float32).T
    sin = np.sin(ang).astype(np.float32).T
    return np.concatenate([cos, cos], 0), np.concatenate([sin, sin], 0)


def rot_matrix():
    r = np.zeros((128, 128), np.float32)
    for m in range(64):
        r[m + 64, m] = -1.0
        r[m, m + 64] = 1.0
    return r


def tile_w(w, ncols_tile=128):
    n = w.shape[1] // ncols_tile
    return np.ascontiguousarray(w.reshape(NCH, 128, n, ncols_tile).transpose(2, 1, 0, 3).reshape(n, 128, NCH * ncols_tile))


def gla_body(P, banks, full, x_sb, xtok, msk_sb, msk_tok, dr):
    nw, wq, wk, wv, wg, wr, wgate, gn, consts = [dr[k_] for k_ in ("nw", "wq", "wk", "wv", "wg", "wr", "wgate", "gn", "consts")]
    QS = 128 ** -0.5
    if True:
        nw_sb = P.sbuf("nw", [128, NCH], F32); nw_tok = Tok("nw")
        cst = P.sbuf("cst", [128, 6 * 128], F32); cst_tok = Tok("cst")
        UF, UB, UFx, UBx, MF, MB = [cst[:, i * 128:(i + 1) * 128] for i in range(6)]
        gn_sb = P.sbuf("gn", [128, 2], F32); gn_tok = Tok("gn")
        wgate32 = P.sbuf("wgate32", [33, 1024], F32); wgate32_tok = Tok("wgate32")
        wgateb = P.sbuf("wgateb", [33, 1024], BF16); wgateb_tok = Tok("wgateb")
        sin_sb = P.sbuf("sin", [128, 8, 256], F32) if full else None
        sin_tok = Tok("sin")
        onesD = P.sbuf("onesD", [128, 128], BF16); onesD_tok = Tok("onesD")
        onesV = P.sbuf("onesV", [128, 128], BF16); onesV_tok = Tok("onesV")
        P.dma("sp", nw_sb[:], nw, writes=[nw_tok])
        P.dma("sp", cst[:], consts, writes=[cst_tok])
        P.dma("sp", gn_sb[:], gn, writes=[gn_tok])
        P.dma("sp", wgate32[:], wgate, writes=[wgate32_tok])
        if full:
            P.dma("sp", sin_sb[:, 0:4, :].rearrange("p a b -> p (a b)"), dr["s_g"][0:128, 0:1024], reads=[dr["s_g_dtok"]], writes=[sin_tok])
            P.dma("sp", sin_sb[:, 4:8, :].rearrange("p a b -> p (a b)"), dr["s_g"][128:256, 1024:2048], reads=[dr["s_g_dtok"]], writes=[sin_tok])
        P.op("dve", lambda e: e.tensor_copy(out=wgateb[:], in_=wgate32[:]), [wgate32_tok], [wgateb_tok])
        P.op("dve", lambda e: e.memset(onesD[:], 1.0 / D), [], [onesD_tok])
        P.op("dve", lambda e: e.memset(onesV[:], 1.0 / 256), [], [onesV_tok])
        hT = P.sbuf("hT", [128, NCH, TC], BF16); htok = [Tok(f"h{t}") for t in range(4)]
        sq_ring = Ring(P, "sq", [128, NCH, 512], BF16, 1)
        rstd_ring = Ring(P, "rstd", [128, 512], F32, 2)
        if full:
            xv = dr["x_spill"].rearrange("(c p) t -> p c t", p=128)
            for tt in range(4):
                sl = slice(tt * 512, (tt + 1) * 512)
                P.dma("sp", xv[:, :, sl], x_sb[:, :, sl], reads=[xtok[c][tt] for c in range(NCH)], writes=[dr["x_spill_dtok"]])
        rmsnorm_fm(P, banks, x_sb, xtok, nw_sb, nw_tok, onesD, onesD_tok, hT, htok, sq_ring, rstd_ring)
        if full:
            P.barrier()
            xa = x_sb[:].rearrange("p c t -> p (c t)").bitcast(BF16)
            sst = [xa[:, d_ * 8192:(d_ + 1) * 8192].rearrange("p (n v) -> p n v", n=32) for d_ in range(2)]
            qdec = [xa[:, 16384 + d_ * 2048:16384 + (d_ + 1) * 2048] for d_ in range(2)]
            kinv = [xa[:, 20480 + d_ * 2048:20480 + (d_ + 1) * 2048] for d_ in range(2)]
            kdec = [xa[:, 24576 + d_ * 2048:24576 + (d_ + 1) * 2048].rearrange("p (i d) -> p i d", i=16) for d_ in range(2)]
            v_sb = xa[:, 28672:32768].rearrange("p (i f) -> p i f", i=16)
        else:
            kdec = [P.sbuf(f"kdec{d_}", [128, 16, 128], BF16)[:] for d_ in range(2)]
            v_sb = P.sbuf("v", [128, 16, 256], BF16)[:]
        sst_tok = [[Tok(f"sst{d_}_{n}") for n in range(32)] for d_ in range(2)]
        ws2_ring = Ring(P, "ws2", [128, 2048], F32, 2)
        wrb = P.sbuf("wrb", [128, NCH, 32], BF16); wrb_tok = Tok("wrb")
        load_cast(P, wr, wrb[:].rearrange("p c f -> p (c f)"), wrb_tok, ws2_ring, eng="act")
        rT = P.sbuf("rT", [33, TC], BF16); rT_tok = Tok("rT")
        P.op("dve", lambda e: e.memset(rT[32:33, :], 1.0), [], [rT_tok])
        for tt in range(4):
            sl = slice(tt * 512, (tt + 1) * 512)
            bk, bkt = banks.next()
            for c in range(NCH):
                P.op("pe", lambda e, bk=bk, c=c, sl=sl: e.matmul(bk[0:32, :], lhsT=wrb[:, c, :], rhs=hT[:, c, sl], start=(c == 0), stop=(c == NCH - 1)), [wrb_tok, htok[tt]], [bkt])
            P.op("act", lambda e, bk=bk, sl=sl: e.copy(out=rT[0:32, sl], in_=bk[0:32, :]), [bkt], [rT_tok])

        wsets = []
        for i_ in range(2):
            wsets.append({"wqb": P.sbuf("wqb", [128, NCH, 128], BF16) if full else None, "wqb_tok": Tok("wqb"),
                          "wkb": P.sbuf("wkb", [128, NCH, 128], BF16), "wkb_tok": Tok("wkb"),
                          "wvb": P.sbuf("wvb", [128, NCH, 256], BF16), "wvb_tok": Tok("wvb"),
                          "wgb": P.sbuf("wgb", [128, NCH, 256], BF16) if full else None, "wgb_tok": Tok("wgb")})
        def load_head(h_):
            if h_ >= 4:
                return
            w_ = wsets[h_ % 2]
            if full:
                load_cast(P, wq[h_], w_["wqb"][:].rearrange("p c f -> p (c f)"), w_["wqb_tok"], ws2_ring, eng="act")
            load_cast(P, wk[h_], w_["wkb"][:].rearrange("p c f -> p (c f)"), w_["wkb_tok"], ws2_ring, eng="act")
            load_cast(P, wv[h_], w_["wvb"][:].rearrange("p c f -> p (c f)"), w_["wvb_tok"], ws2_ring, eng="act")
            if full:
                load_cast(P, wg[h_], w_["wgb"][:].rearrange("p c f -> p (c f)"), w_["wgb_tok"], ws2_ring, eng="act")

        load_head(0)
        kdec_tok = [[Tok(f"kdec{d_}_{t}") for t in range(4)] for d_ in range(2)]
        v_tok = [Tok(f"v{t}") for t in range(4)]
        dec = [P.sbuf(f"dec{d_}", [128, 32], F32) for d_ in range(2)]
        dec_tok = [[Tok(f"dec{d_}_{t}") for t in range(4)] for d_ in range(2)]
        if full:
            qk_tok = [[Tok(f"qk{d_}_{t}") for t in range(4)] for d_ in range(2)]
        S = [P.sbuf(f"S{d_}", [128, 256], F32) for d_ in range(2)]
        S_tok = [Tok(f"S{d_}") for d_ in range(2)]
        sqv_ = sq_ring.aps[0][:].rearrange("p c t -> p (c t)").bitcast(F32)
        ez_ring = Ring.__new__(Ring)
        ez_ring.aps = [sqv_[:, 0:1024].rearrange("p (a b) -> p a b", a=2)]; ez_ring.toks = [Tok("ez")]; ez_ring.i = 0; ez_ring.n = 1
        E_ring = Ring(P, "E", [128, 4, 512], F32, 1)
        EK_ring = Ring.__new__(Ring)
        EK_ring.aps = [sqv_[:, 1024:2048].rearrange("p (a b) -> p a b", a=2)]; EK_ring.toks = [Tok("EK")]; EK_ring.i = 0; EK_ring.n = 1
        if full:
            att_ring = Ring(P, "att", [128, 256], BF16, 4)
            o_ring = Ring(P, "oh", [128, 2, 512], F32, 1)
            sqo_ring = Ring(P, "sqo", [128, 2, 512], BF16, 1)
            rso_ring = Ring(P, "rso", [128, 512], F32, 1)
            sg_ring = Ring(P, "sg", [128, 2, 512], F32, 1)
            of_ring = Ring(P, "of", [128, 2, 512], BF16, 2)
        sout_sb = None
        if not full:
            sout_sb = P.sbuf("sout", [128, 8, 256], F32); sout_tok = Tok("sout")

        for h in range(4):
            if full:
                for dr_ in range(2):
                    P.dma("sp", kdec[dr_].rearrange("p a b -> p (a b)"), dr["kv_s"][h][:, dr_ * 2048:(dr_ + 1) * 2048], reads=[dr["kv_s_dtok"][h]], writes=kdec_tok[dr_])
                P.dma("sp", v_sb.rearrange("p a b -> p (a b)"), dr["kv_s"][h][:, 4096:8192], reads=[dr["kv_s_dtok"][h]], writes=v_tok)
            w_ = wsets[h % 2]
            wqb, wqb_tok, wkb, wkb_tok = w_["wqb"], w_["wqb_tok"], w_["wkb"], w_["wkb_tok"]
            wvb, wvb_tok, wgb, wgb_tok = w_["wvb"], w_["wvb_tok"], w_["wgb"], w_["wgb_tok"]
            for tt in range(4):
                sl = slice(tt * 512, (tt + 1) * 512)
                zb = [banks.next() for _ in range(2)]
                for dr_ in range(2):
                    for i4 in range(4):
                        ti = tt * 4 + i4
                        P.op("pe", lambda e, dr_=dr_, i4=i4, ti=ti, zb=zb, h=h: e.matmul(
                            zb[dr_][0][:, i4 * 128:(i4 + 1) * 128], lhsT=rT[0:33, ti * 128:(ti + 1) * 128],
                            rhs=wgateb[0:33, dr_ * 512 + h * 128:dr_ * 512 + (h + 1) * 128], start=True, stop=True), [rT_tok, wgateb_tok], [zb[dr_][1]])
                ez, ezt = ez_ring.next()
                sp, spt = ez, ezt
                for dr_ in range(2):
                    P.op("act", lambda e, ez=ez, dr_=dr_, zb=zb: e.activation(out=ez[:, dr_, :], in_=zb[dr_][0][:], func=AF.Exp, scale=-1.0), [zb[dr_][1]], [ezt])
                P.op("act", lambda e, ez=ez, sp=sp: e.activation(out=sp[:], in_=ez[:], func=AF.Ln, bias=1.0, scale=1.0), [ezt], [spt])
                if not full:
                    kb = [banks.next() for _ in range(2)]
                    for dr_ in range(2):
                        Ux = UFx if dr_ == 0 else UBx
                        for i4 in range(4):
                            P.op("pe", lambda e, dr_=dr_, i4=i4, kb=kb, Ux=Ux, sp=sp: e.matmul(kb[dr_][0][:, i4 * 128:(i4 + 1) * 128], lhsT=Ux, rhs=sp[:, dr_, i4 * 128:(i4 + 1) * 128],
                                                                                         start=True, stop=True), [cst_tok, spt], [kb[dr_][1]])
                    EK, EKt = EK_ring.next()
                    for dr_ in range(2):
                        P.op("act", lambda e, EK=EK, dr_=dr_, kb=kb: e.activation(out=EK[:, dr_, :], in_=kb[dr_][0][:], func=AF.Exp, scale=-1.0 / 16.0), [kb[dr_][1]], [EKt])
                fb = [banks.next() for _ in range(2)]
                for dr_ in range(2):
                    U = UF if dr_ == 0 else UB
                    for i4 in range(4):
                        P.op("pe", lambda e, dr_=dr_, i4=i4, fb=fb, U=U, sp=sp: e.matmul(fb[dr_][0][:, i4 * 128:(i4 + 1) * 128], lhsT=sp[:, dr_, i4 * 128:(i4 + 1) * 128], rhs=U,
                                                                                   start=True, stop=True), [cst_tok, spt], [fb[dr_][1]])
                E, Et = E_ring.next()
                for dr_ in range(2):
                    P.op("act", lambda e, E=E, dr_=dr_, fb=fb: e.activation(out=E[:, 2 * dr_, :], in_=fb[dr_][0][:], func=AF.Exp, scale=-1.0 / 16.0), [fb[dr_][1]], [Et])
                    if full:
                        P.op("act", lambda e, E=E, dr_=dr_, fb=fb: e.activation(out=E[:, 2 * dr_ + 1, :], in_=fb[dr_][0][:], func=AF.Exp, scale=1.0 / 16.0), [fb[dr_][1]], [Et])
                P.op("dve", lambda e, E=E, tt=tt: e.tensor_copy(out=dec[0][:, tt * 8:(tt + 1) * 8], in_=E[:, 0, 63:512:64]), [Et], [dec_tok[0][tt]])
                P.op("dve", lambda e, E=E, tt=tt: e.tensor_copy(out=dec[1][:, tt * 8:(tt + 1) * 8], in_=E[:, 2, 0:512:64]), [Et], [dec_tok[1][tt]])
                if not full:
                    kt_b = banks.next()
                    for i4 in range(4):
                        ti = tt * 4 + i4
                        for c in range(NCH):
                            P.op("pe", lambda e, kt_b=kt_b, i4=i4, ti=ti, c=c, wkb=wkb: e.matmul(kt_b[0][:, i4 * 128:(i4 + 1) * 128], lhsT=hT[:, c, ti * 128:(ti + 1) * 128], rhs=wkb[:, c, :],
                                                                                    start=(c == 0), stop=(c == NCH - 1)), [htok[tt], wkb_tok], [kt_b[1]])
                    for dr_ in range(2):
                        P.op("dve", lambda e, dr_=dr_, kt_b=kt_b, EK=EK, tt=tt: e.tensor_tensor(out=kdec[dr_][:, tt * 4:(tt + 1) * 4, :].rearrange("p a b -> p (a b)"),
                                                                                             in0=kt_b[0][:], in1=EK[:, dr_, :], op=ALU.mult), [kt_b[1], EKt], [kdec_tok[dr_][tt]])
                    for i2 in range(2):
                        vb = banks.next()
                        for i1 in range(2):
                            ti = tt * 4 + i2 * 2 + i1
                            for c in range(NCH):
                                P.op("pe", lambda e, vb=vb, i1=i1, ti=ti, c=c, wvb=wvb: e.matmul(vb[0][:, i1 * 256:(i1 + 1) * 256], lhsT=hT[:, c, ti * 128:(ti + 1) * 128], rhs=wvb[:, c, :],
                                                                                     start=(c == 0), stop=(c == NCH - 1)), [htok[tt], wvb_tok], [vb[1]])
                        t0_ = tt * 4 + i2 * 2
                        P.op("act", lambda e, vb=vb, t0_=t0_: e.copy(out=v_sb[:, t0_:t0_ + 2, :].rearrange("p a b -> p (a b)"), in_=vb[0][:]), [vb[1]], [v_tok[tt]])
                    for dr_ in range(2):
                        P.dma("sp", dr["kv_s"][h][:, dr_ * 2048 + tt * 512:dr_ * 2048 + (tt + 1) * 512], kdec[dr_][:, tt * 4:(tt + 1) * 4, :].rearrange("p a b -> p (a b)"),
                              reads=[kdec_tok[dr_][tt]], writes=[dr["kv_s_dtok"][h]])
                    P.dma("sp", dr["kv_s"][h][:, 4096 + tt * 1024:4096 + (tt + 1) * 1024], v_sb[:, tt * 4:(tt + 1) * 4, :].rearrange("p a b -> p (a b)"),
                          reads=[v_tok[tt]], writes=[dr["kv_s_dtok"][h]])
                if full:
                    qb_ = banks.next()
                    kb2 = banks.next()
                    for c in range(NCH):
                        P.op("pe", lambda e, qb_=qb_, c=c, sl=sl, wqb=wqb: e.matmul(qb_[0][:], lhsT=wqb[:, c, :], rhs=hT[:, c, sl], start=(c == 0), stop=(c == NCH - 1)), [wqb_tok, htok[tt]], [qb_[1]])
                    for c in range(NCH):
                        P.op("pe", lambda e, kb2=kb2, c=c, sl=sl, wkb=wkb: e.matmul(kb2[0][:], lhsT=wkb[:, c, :], rhs=hT[:, c, sl], start=(c == 0), stop=(c == NCH - 1)), [wkb_tok, htok[tt]], [kb2[1]])
                    for dr_ in range(2):
                        P.op("dve", lambda e, dr_=dr_, qb_=qb_, E=E, sl=sl: e.scalar_tensor_tensor(out=qdec[dr_][:, sl], in0=qb_[0][:], scalar=QS, in1=E[:, 2 * dr_, :],
                                                                                              op0=ALU.mult, op1=ALU.mult), [qb_[1], Et], [qk_tok[dr_][tt]])
                        P.op("dve", lambda e, dr_=dr_, kb2=kb2, E=E, sl=sl: e.tensor_tensor(out=kinv[dr_][:, sl], in0=kb2[0][:], in1=E[:, 2 * dr_ + 1, :], op=ALU.mult),
                             [kb2[1], Et], [qk_tok[dr_][tt]])
            load_head(h + 1)
            for dr_ in range(2):
                if full:
                    P.op("act", lambda e, dr_=dr_, h=h: e.activation(out=S[dr_][:], in_=sin_sb[:, dr_ * 4 + h, :], func=AF.Identity, scale=msk_sb[:, 1 - dr_:2 - dr_]),
                         [sin_tok, msk_tok], [S_tok[dr_]])
                else:
                    P.op("dve", lambda e, dr_=dr_: e.memset(S[dr_][:], 0.0), [], [S_tok[dr_]])
            orders = [list(range(32)), list(range(31, -1, -1))]
            for step in range(32):
                for dr_ in range(2):
                    n = orders[dr_][step]
                    ti, half = n // 2, n % 2
                    tt = ti // 4
                    rows = slice(half * 64, (half + 1) * 64)
                    if full:
                        P.op("act", lambda e, dr_=dr_, n=n: e.copy(out=sst[dr_][:, n, :], in_=S[dr_][:]), [S_tok[dr_]], [sst_tok[dr_][n]])
                    if full and step == 31:
                        continue
                    bk, bkt = banks.next()
                    P.op("pe", lambda e, bk=bk, dr_=dr_, ti=ti, rows=rows: e.matmul(bk[:, 0:256], lhsT=kdec[dr_][rows, ti, :], rhs=v_sb[rows, ti, :], start=True, stop=True),
                         [kdec_tok[dr_][tt], v_tok[tt]], [bkt])
                    P.op("dve", lambda e, bk=bk, dr_=dr_, n=n: e.scalar_tensor_tensor(out=S[dr_][:], in0=S[dr_][:], scalar=dec[dr_][:, n:n + 1], in1=bk[:, 0:256],
                                                                                    op0=ALU.mult, op1=ALU.add), [S_tok[dr_], dec_tok[dr_][tt], bkt], [S_tok[dr_]])
            if not full:
                for dr_ in range(2):
                    P.op("dve", lambda e, dr_=dr_, h=h: e.tensor_copy(out=sout_sb[:, dr_ * 4 + h, :], in_=S[dr_][:]), [S_tok[dr_]], [sout_tok])
            if not full:
                continue
            def att_stage(tt, i4):
                ti = tt * 4 + i4
                tsl = slice(ti * 128, (ti + 1) * 128)
                ab, abt = banks.next()
                for dr_ in range(2):
                    P.op("pe", lambda e, ab=ab, dr_=dr_, tsl=tsl: e.matmul(ab[:, dr_ * 128:(dr_ + 1) * 128], lhsT=kinv[dr_][:, tsl], rhs=qdec[dr_][:, tsl], start=True, stop=True),
                         [qk_tok[dr_][tt]], [abt])
                at, att_t = att_ring.next()
                P.op("dve", lambda e, at=at, ab=ab: e.tensor_tensor(out=at[:, 0:128], in0=ab[:, 0:128], in1=MF, op=ALU.mult), [abt, cst_tok], [att_t])
                P.op("dve", lambda e, at=at, ab=ab: e.tensor_tensor(out=at[:, 128:256], in0=ab[:, 128:256], in1=MB, op=ALU.mult), [abt, cst_tok], [att_t])
                return at, att_t

            oitems = [(tt, i4) for tt in range(4) for i4 in range(4)]
            att_next = att_stage(*oitems[0])
            ob = None
            for k_, (tt, i4) in enumerate(oitems):
                sl = slice(tt * 512, (tt + 1) * 512)
                at, att_t = att_next
                if k_ + 1 < len(oitems):
                    att_next = att_stage(*oitems[k_ + 1])
                if i4 == 0:
                    ob = [banks.next() for _ in range(2)]
                ti = tt * 4 + i4
                for j in range(2):
                    osl = slice(i4 * 128, (i4 + 1) * 128)
                    vsl = slice(j * 128, (j + 1) * 128)
                    P.op("pe", lambda e, ob=ob, j=j, osl=osl, vsl=vsl, ti=ti, at=at: e.matmul(ob[j][0][:, osl], lhsT=v_sb[:, ti, vsl], rhs=at[:, 0:128], start=True, stop=False),
                         [v_tok[tt], att_t], [ob[j][1]])
                    P.op("pe", lambda e, ob=ob, j=j, osl=osl, vsl=vsl, ti=ti, at=at: e.matmul(ob[j][0][:, osl], lhsT=v_sb[:, ti, vsl], rhs=at[:, 128:256], start=False, stop=False),
                         [v_tok[tt], att_t], [ob[j][1]])
                    for dr_ in range(2):
                        for half in range(2):
                            n = 2 * ti + half
                            c0 = i4 * 128 + half * 64
                            q0 = ti * 128 + half * 64
                            lastmm = (dr_ == 1 and half == 1)
                            P.op("pe", lambda e, ob=ob, j=j, c0=c0, q0=q0, dr_=dr_, n=n, vsl=vsl, lastmm=lastmm: e.matmul(
                                ob[j][0][:, c0:c0 + 64], lhsT=sst[dr_][:, n, vsl], rhs=qdec[dr_][:, q0:q0 + 64], start=False, stop=lastmm),
                                [sst_tok[dr_][n], qk_tok[dr_][tt]], [ob[j][1]])
                if i4 != 3:
                    continue
                sg, sgt = sg_ring.next()
                for j in range(2):
                    gb = banks.next()
                    for c in range(NCH):
                        P.op("pe", lambda e, gb=gb, c=c, j=j, sl=sl, wgb=wgb: e.matmul(gb[0][:], lhsT=wgb[:, c, j * 128:(j + 1) * 128], rhs=hT[:, c, sl], start=(c == 0), stop=(c == NCH - 1)),
                             [wgb_tok, htok[tt]], [gb[1]])
                    P.op("act", lambda e, sg=sg, j=j, gb=gb: e.activation(out=sg[:, j, :], in_=gb[0][:], func=AF.Silu), [gb[1]], [sgt])
                oh, oht = o_ring.next()
                for j in range(2):
                    P.op("act", lambda e, oh=oh, j=j, ob=ob: e.copy(out=oh[:, j, :], in_=ob[j][0][:]), [ob[j][1]], [oht])
                sqo, sqot = sqo_ring.next()
                P.op("act", lambda e, sqo=sqo, oh=oh: e.activation(out=sqo[:], in_=oh[:], func=AF.Square), [oht], [sqot])
                mb, mbt = banks.next()
                for j in range(2):
                    P.op("pe", lambda e, mb=mb, sqo=sqo, j=j: e.matmul(mb[:], lhsT=onesV[:], rhs=sqo[:, j, :], start=(j == 0), stop=(j == 1)), [onesV_tok, sqot], [mbt])
                rso, rsot = rso_ring.next()
                P.op("act", lambda e, rso=rso, mb=mb: e.activation(out=rso[:], in_=mb[:], func=AF.Ln, bias=EPS, scale=1.0), [mbt], [rsot])
                P.op("act", lambda e, rso=rso: e.activation(out=rso[:], in_=rso[:], func=AF.Exp, scale=-0.5), [rsot], [rsot])
                on, ont = oh, oht
                for j in range(2):
                    P.op("dve", lambda e, on=on, oh=oh, j=j, rso=rso: e.scalar_tensor_tensor(out=on[:, j, :], in0=oh[:, j, :], scalar=gn_sb[:, j:j + 1], in1=rso[:],
                                                                                          op0=ALU.mult, op1=ALU.mult), [oht, gn_tok, rsot], [ont])
                of, oft = of_ring.next()
                P.op("pool", lambda e, of=of, on=on, sg=sg: e.tensor_tensor(out=of[:], in0=on[:], in1=sg[:], op=ALU.mult), [ont, sgt], [oft])
                P.dma("sp", dr["oT"][h * 256:(h + 1) * 256, sl].rearrange("(j p) t -> p j t", p=128), of[:], reads=[oft], writes=[dr["oT_dtok"]])
        if not full:
            P.dma("sp", dr["s_out"], sout_sb[:].rearrange("p a b -> p (a b)"), reads=[sout_tok], writes=[dr["s_out_dtok"]])


def oproj_body(P, banks, x_sb, xtok, dr):
    if True:
        oT = P.sbuf("oT", [128, NCH, TC], BF16); o_tok = [Tok(f"o{t}") for t in range(4)]
        ov = dr["oT"].rearrange("(c p) t -> p c t", p=128)
        ws_ring = Ring(P, "ws", [128, 1024], F32, 2)
        wob = P.sbuf("wob", [128, NCH, D], BF16); wob_tok = [Tok(f"wob{c}") for c in range(NCH)]
        P.dma("sp", oT[:, :, 0:512], ov[:, :, 0:512], reads=[dr["oT_dtok"]], writes=[o_tok[0]])
        for c in range(NCH):
            load_cast(P, dr["wo"][c], wob[:, c, :], wob_tok[c], ws_ring, eng="dve")
        for tt in range(1, 4):
            sl = slice(tt * 512, (tt + 1) * 512)
            P.dma("sp", oT[:, :, sl], ov[:, :, sl], reads=[dr["oT_dtok"]], writes=[o_tok[tt]])
        if dr.get("x_spill") is not None:
            xv = dr["x_spill"].rearrange("(c p) t -> p c t", p=128)
            for tt in range(4):
                sl = slice(tt * 512, (tt + 1) * 512)
                P.dma("sp", x_sb[:, :, sl], xv[:, :, sl], reads=[dr["x_spill_dtok"]], writes=[xtok[c][tt] for c in range(NCH)])
        for tt in range(4):
            for d in range(NCH):
                sl = slice(tt * 512, (tt + 1) * 512)
                bk, bkt = banks.next()
                for c in range(NCH):
                    P.op("pe", lambda e, bk=bk, c=c, d=d, sl=sl: e.matmul(bk[:], lhsT=wob[:, c, d * 128:(d + 1) * 128], rhs=oT[:, c, sl], start=(c == 0), stop=(c == NCH - 1)),
                         [wob_tok[c], o_tok[tt]], [bkt])
                P.op("dve", lambda e, bk=bk, d=d, sl=sl: e.tensor_tensor(out=x_sb[:, d, sl], in0=x_sb[:, d, sl], in1=bk[:], op=ALU.add), [bkt, xtok[d][tt]], [xtok[d][tt]])


def gla_consts():
    s = np.arange(128)[:, None]
    t = np.arange(128)[None, :]
    same = (s // 64) == (t // 64)
    g = np.float32(1.0)
    UF = np.where(same & (s <= t), g, 0).astype(np.float32)
    UB = np.where(same & (s >= t), g, 0).astype(np.float32)
    UFx = np.where(same & (s > t), g, 0).astype(np.float32)
    UBx = np.where(same & (s < t), g, 0).astype(np.float32)
    MF = np.where(same & (s <= t), 1, 0).astype(np.float32)
    MB = np.where(same & (s > t), 1, 0).astype(np.float32)
    return np.ascontiguousarray(np.concatenate([UF, UB, UFx, UBx, MF, MB], axis=1))


def pcf(w):
    n = w.shape[1]
    return np.ascontiguousarray(w.reshape(NCH, 128, n).transpose(1, 0, 2).reshape(128, NCH * n))


NL_DEFAULT = 4


def build_fused(nl=NL_DEFAULT):
    nc = bass.Bass("TRN2", target_bir_lowering=False)
    di = lambda n, s, dt=F32: nc.dram_tensor(n, list(s), dt, kind="ExternalInput").ap()
    dint = lambda n, s, dt=F32: nc.dram_tensor(n, list(s), dt, kind="Internal").ap()
    xT = di("xT", [D, TC])
    msk = di("msk", [128, 2])
    cs = dint("cs_scr", [128, 2 * TC])
    rot = di("rot", [128, 128])
    consts = di("consts", [128, 6 * 128])
    yT = nc.dram_tensor("yT", [D, TC], F32, kind="ExternalOutput").ap()
    L = []
    for i in range(nl):
        d = {"nwm": di(f"nwm{i}", [128, NCH]), "nwf": di(f"nwf{i}", [128, NCH]),
             "wup": di(f"wup{i}", [NFT, 128, 1024]), "cw": di(f"cw{i}", [128, NFT * 4]), "wdn": di(f"wdn{i}", [NPAIR, 128, 1024]),
             "wo": di(f"wo{i}", [NCH, 128, D]),
             "hb": dint(f"hb{i}", [128, 16]), "hg": dint(f"hg{i}", [256, 16])}
        if i % 2 == 0:
            d.update({"wq": di(f"wq{i}", [4, 128, 1024]), "wk": di(f"wk{i}", [4, 128, 1024]), "wv": di(f"wv{i}", [4, 128, 2048]),
                      "wg": di(f"wg{i}", [4, 128, 2048]), "wr": di(f"wr{i}", [128, 256]), "wgate": di(f"wgate{i}", [33, 1024]),
                      "gn": di(f"gn{i}", [128, 2]),
                      "st_b": dint(f"stb{i}", [128, 2048]), "st_g": dint(f"stg{i}", [256, 2048]),
                      "oT": dint(f"oTs{i}", [D, TC], BF16), "x_spill": dint(f"xsp{i}", [D, TC]),
                      "kv_s": dint(f"kvs{i}", [4, 128, 8192], BF16)})
        else:
            d.update({"wqk": di(f"wqk{i}", [10, 128, 1024]), "wv": di(f"wva{i}", [128, NCH * 256]), "gains": di(f"gains{i}", [128, 2]),
                      "q_s": dint(f"qs{i}", [8, 128, TC], BF16),
                      "kv_b": dint(f"kvb{i}", [512, TC], BF16), "kv_g": dint(f"kvg{i}", [1024, TC], BF16),
                      "oT": dint(f"oTs{i}", [D, TC], BF16)})
        L.append(d)
    with contextlib.ExitStack() as es:
        P = Prog(nc, es)
        bank_aps = [P.psum(f"bk{i}", [128, 512]) for i in range(8)]
        bank_toks = [Tok(f"bk{i}") for i in range(8)]
        banks8 = Banks(P, 8, aps=bank_aps, toks=bank_toks)
        banks6 = Banks(P, 6, aps=bank_aps[0:6], toks=bank_toks[0:6])
        hbanks = Banks(P, 2, aps=bank_aps[6:8], toks=bank_toks[6:8])
        sbanks = Banks(P, 3, aps=bank_aps[0:3], toks=bank_toks[0:3])
        obanks = Banks(P, 2, aps=bank_aps[3:5], toks=bank_toks[3:5])
        dbanks = Banks(P, 2, aps=bank_aps[5:7], toks=bank_toks[5:7])
        x_sb = P.sbuf("x", [128, NCH, TC], F32)
        xtok = [[Tok(f"x{c}_{t}") for t in range(4)] for c in range(NCH)]
        msk_sb = P.sbuf("msk", [128, 2], F32); msk_tok = Tok("msk")
        P.dma("sp", msk_sb[:], msk[:, :], writes=[msk_tok])
        load_x(P, xT, x_sb, xtok)
        cs_dtok = Tok("cs_d")
        with P.phase():
            cs_gen = P.sbuf("csgen", [128, 2 * TC], F32); cs_gen_tok = Tok("csgen")
            rope_gen(P, cs_gen, cs_gen_tok, msk_sb, msk_tok)
            P.dma("sp", cs[:, :], cs_gen[:], reads=[cs_gen_tok], writes=[cs_dtok])
        for i in range(nl):
            d = L[i]
            wo_list = [d["wo"][c] for c in range(NCH)]
            if i % 2 == 0:
                stb_tok, stg_tok, oT_tok, xsp_tok = Tok(f"stb{i}"), Tok(f"stg{i}"), Tok(f"oTd{i}"), Tok(f"xsp{i}")
                base = {"nw": d["nwm"][:, :], "wq": [d["wq"][h] for h in range(4)], "wk": [d["wk"][h] for h in range(4)],
                        "wv": [d["wv"][h] for h in range(4)], "wg": [d["wg"][h] for h in range(4)], "wr": d["wr"][:, :],
                        "wgate": d["wgate"][:, :], "gn": d["gn"][:, :], "consts": consts[:, :],
                        "kv_s": [d["kv_s"][h] for h in range(4)], "kv_s_dtok": [Tok(f"kvs{i}_{h}") for h in range(4)]}
                with P.phase():
                    gla_body(P, banks8, False, x_sb, xtok, msk_sb, msk_tok, dict(base, s_out=d["st_b"][:, :], s_out_dtok=stb_tok))
                P.cc(d["st_g"][:, :], d["st_b"][:, :], reads=[stb_tok], writes=[stg_tok])
                with P.phase():
                    gla_body(P, banks8, True, x_sb, xtok, msk_sb, msk_tok,
                             dict(base, s_g=d["st_g"], s_g_dtok=stg_tok, oT=d["oT"], oT_dtok=oT_tok, x_spill=d["x_spill"], x_spill_dtok=xsp_tok))
                with P.phase():
                    oproj_body(P, banks8, x_sb, xtok, {"oT": d["oT"], "oT_dtok": oT_tok, "wo": wo_list, "x_spill": d["x_spill"], "x_spill_dtok": xsp_tok})
            else:
                q_tok, kvb_tok, kvg_tok = Tok(f"qd{i}"), Tok(f"kvb{i}"), Tok(f"kvg{i}")
                kvb, kvg = d["kv_b"], d["kv_g"]
                dr1 = {"nw": d["nwm"][:, :], "wqk": [d["wqk"][f] for f in range(10)], "wv": d["wv"][:, :], "gains": d["gains"][:, :],
                       "cs": cs[:, :], "cs_dtok": cs_dtok, "rot": rot[:, :],
                       "q_out": [d["q_s"][h] for h in range(8)], "k_out": [kvb[h * 128:(h + 1) * 128, :] for h in range(2)],
                       "v_out": kvb[256:512, :].rearrange("r (a f) -> (r a) f", f=256),
                       "q_dtok": q_tok, "kv_dtok": kvb_tok}
                dr1["after_kv"] = lambda kvg=kvg, kvb=kvb, kvb_tok=kvb_tok, kvg_tok=kvg_tok: P.cc(kvg[:, :], kvb[:, :], reads=[kvb_tok], writes=[kvg_tok])
                with P.phase():
                    attn1_body(P, banks8, x_sb, xtok, dr1)
                dr2 = {"q": [d["q_s"][h] for h in range(8)],
                       "k": [[kvg[r * 512 + h * 128:r * 512 + (h + 1) * 128, :] for r in range(2)] for h in range(2)],
                       "v": [kvg[r * 512 + 256:r * 512 + 512, :].rearrange("r (a f) -> (r a) f", f=256) for r in range(2)],
                       "wo": wo_list, "q_dtok": q_tok, "kvg_dtok": kvg_tok}
                with P.phase():
                    attn2_body(P, sbanks, obanks, dbanks, x_sb, xtok, dr2)
            hb_tok, hg_tok = Tok(f"hb{i}"), Tok(f"hg{i}")
            with P.phase():
                hb_sb = P.sbuf("hb", [128, 16], F32); hbs_tok = Tok("hbs")
                hg_sb = P.sbuf("hg", [128, 2, 16], F32); hgs_tok = Tok("hgs")
                xh_sb = P.sbuf("xh", [128, NCH, 2], F32); xh_tok = Tok("xh")
                P.op("dve", lambda e, hb_sb=hb_sb: e.tensor_copy(out=hb_sb[:, 0:16:2], in_=x_sb[:, :, 0]), [xtok[c][0] for c in range(NCH)], [hbs_tok])
                P.op("dve", lambda e, hb_sb=hb_sb: e.tensor_copy(out=hb_sb[:, 1:16:2], in_=x_sb[:, :, TC - 1]), [xtok[c][3] for c in range(NCH)], [hbs_tok])
                P.dma("sp", d["hb"][:, :], hb_sb[:], reads=[hbs_tok], writes=[hb_tok])
                P.cc(d["hg"][:, :], d["hb"][:, :], reads=[hb_tok], writes=[hg_tok])
                P.dma("sp", hg_sb[:], d["hg"].rearrange("(r p) e -> p r e", p=128), reads=[hg_tok], writes=[hgs_tok])
                P.op("act", lambda e, xh_sb=xh_sb, hg_sb=hg_sb: e.activation(out=xh_sb[:, :, 0], in_=hg_sb[:, 0, 1:16:2], func=AF.Identity, scale=msk_sb[:, 1:2]),
                     [hgs_tok, msk_tok], [xh_tok])
                P.op("act", lambda e, xh_sb=xh_sb, hg_sb=hg_sb: e.activation(out=xh_sb[:, :, 1], in_=hg_sb[:, 1, 0:16:2], func=AF.Identity, scale=msk_sb[:, 0:1]),
                     [hgs_tok, msk_tok], [xh_tok])
                drf = {"nw": d["nwf"][:, :], "cw": d["cw"][:, :], "wup": [d["wup"][f] for f in range(NFT)], "wdn": [d["wdn"][j] for j in range(NPAIR)]}
                ffn_body(P, banks6, hbanks, x_sb, xtok, xh_sb, xh_tok, drf)
        store_x(P, yT, x_sb, xtok)
        P.final_wait()
        P.emit()
        print("fused program:", P.ninstr, "waits", P.nwaits, "sems", P.nsem)
    return nc


def fused_inputs(nl, x, norm_mix, norm_ffn, gla_w_in, gla_w_gate_up_f, gla_b_gate_f, gla_w_gate_up_b, gla_b_gate_b,
                 gla_norm, gla_w_out, attn_w_qkv, attn_q_norm, attn_k_norm, attn_w_out,
                 ffn_w_up, ffn_w_conv, ffn_b_conv, ffn_w_down):
    f = lambda a: np.asarray(a, dtype=np.float32)
    x = f(x)
    shared = {"rot": rot_matrix(), "consts": gla_consts()}
    for i in range(nl):
        j = i // 2
        wup, cw, wdn = ffn_weights_layout(f(ffn_w_up[i]), f(ffn_w_conv[i]), f(ffn_b_conv[i]), f(ffn_w_down[i]))
        shared.update({f"nwm{i}": fm(f(norm_mix[i])), f"nwf{i}": fm(f(norm_ffn[i])), f"wup{i}": wup, f"cw{i}": cw, f"wdn{i}": wdn})
        if i % 2 == 0:
            w_in = f(gla_w_in[j])
            wgate = np.zeros((33, 1024), np.float32)
            wgate[0:16, 0:512] = f(gla_w_gate_up_f[j]); wgate[32, 0:512] = f(gla_b_gate_f[j])
            wgate[16:32, 512:1024] = f(gla_w_gate_up_b[j]); wgate[32, 512:1024] = f(gla_b_gate_b[j])
            shared.update({
                f"wq{i}": np.stack([pcf(w_in[:, h * 128:(h + 1) * 128]) for h in range(4)]),
                f"wk{i}": np.stack([pcf(w_in[:, 512 + h * 128:512 + (h + 1) * 128]) for h in range(4)]),
                f"wv{i}": np.stack([pcf(w_in[:, 1024 + h * 256:1024 + (h + 1) * 256]) for h in range(4)]),
                f"wg{i}": np.stack([pcf(w_in[:, 2048 + h * 256:2048 + (h + 1) * 256]) for h in range(4)]),
                f"wr{i}": pcf(w_in[:, 3072:3104]), f"wgate{i}": wgate,
                f"gn{i}": np.ascontiguousarray(f(gla_norm[j]).reshape(2, 128).T),
                f"wo{i}": np.ascontiguousarray(f(gla_w_out[j]).reshape(NCH, 128, D))})
        else:
            w_qkv = f(attn_w_qkv[j])
            shared.update({
                f"wqk{i}": tile_w(w_qkv[:, :1280]),
                f"wva{i}": np.ascontiguousarray(w_qkv[:, 1280:].reshape(NCH, 128, 256).transpose(1, 0, 2).reshape(128, NCH * 256)),
                f"gains{i}": np.ascontiguousarray(np.stack([f(attn_q_norm[j]), f(attn_k_norm[j])], 1)),
                f"wo{i}": np.ascontiguousarray(f(attn_w_out[j]).reshape(NCH, 128, D))})
    in_maps = []
    for c in range(NCORES):
        hs = slice((c % 2) * TC, (c % 2 + 1) * TC)
        m = np.zeros((128, 2), np.float32)
        m[:, c % 2] = 1.0
        dct = dict(shared)
        dct["xT"] = np.ascontiguousarray(x[c // 2, hs].T)
        dct["msk"] = m
        in_maps.append(dct)
    return in_maps


def kernel(**inputs):
    nc = get_prog("fused", build_fused)
    in_maps = fused_inputs(NL_DEFAULT, **inputs)
    res = run_bass_kernel_spmd(nc, in_maps, core_ids=list(range(NCORES))).results
    out = np.empty((4, SEQ, D), np.float32)
    for c in range(NCORES):
        out[c // 2, (c % 2) * TC:(c % 2 + 1) * TC] = np.asarray(res[c]["yT"]).T
    return out
```

```python
import contextlib
import math
import numpy as np
import concourse.bass as bass
import concourse.mybir as mybir
from concourse.bass_utils import run_bass_kernel_spmd

F32 = mybir.dt.float32
BF16 = mybir.dt.bfloat16
AF = mybir.ActivationFunctionType
ALU = mybir.AluOpType
ENGS = ("pe", "act", "dve", "pool", "sp")
EPOCH = 30000

NCORES = 8
D = 1024
NCH = 8
TC = 2048
SEQ = 4096
DFF = 2816
NFT = 44
NPAIR = 22
EPS = 1e-6


class Tok:
    __slots__ = ("name", "w", "r", "rd", "dsem", "dcount")

    def __init__(self, name=""):
        self.name = name
        self.w = None
        self.r = {}
        self.rd = []
        self.dsem = None
        self.dcount = 0


class Ins:
    __slots__ = ("eng", "fn", "deps", "dma", "sig", "signaled", "stok", "inc")

    def __init__(self, eng, fn, dma):
        self.eng = eng
        self.fn = fn
        self.dma = dma
        self.deps = None
        self.sig = None
        self.signaled = False
        self.stok = None


class Prog:
    def __init__(self, nc, es):
        self.nc = nc
        self.es = es
        self.streams = {e: [] for e in ENGS}
        self.all = []
        self.nsem = 0
        self.dmas = []
        self.log = None
        self.alloc_es = es
        self.free_sems = []
        self.phase_toks = []
        self.recent_dmas = []
        self.nname = 0

    def new_sem(self, name):
        self.nsem += 1
        return self.es.enter_context(self.nc.semaphore(f"s{self.nsem}_{name}"))

    def sbuf(self, name, shape, dt):
        self.nname += 1
        return self.alloc_es.enter_context(self.nc.sbuf_tensor(f"sb{self.nname}_" + name, list(shape), dt))

    @contextlib.contextmanager
    def phase(self):
        st = contextlib.ExitStack()
        prev = self.alloc_es
        self.alloc_es = st
        self.phase_toks = []
        try:
            yield
        finally:
            self.barrier()
            for t in self.phase_toks:
                self.free_sems.append((t.dsem, t.dcount))
                t.dsem = None
            self.phase_toks = []
            st.close()
            self.alloc_es = prev

    def barrier(self):
        deps = set(self.recent_dmas)
        for e in ENGS:
            for ins in reversed(self.streams[e]):
                if ins.fn is not None and not ins.dma:
                    deps.add(ins)
                    break
        self.recent_dmas = []
        for e in ENGS:
            b = Ins(e, None, False)
            b.deps = set(deps)
            self.streams[e].append(b)
            self.all.append(b)

    def psum(self, name, shape, dt=F32):
        return self.es.enter_context(self.nc.psum_tensor("ps_" + name, list(shape), dt))

    def op(self, eng, fn, reads=(), writes=(), dma=False, inc=16):
        ins = Ins(eng, fn, dma)
        ins.inc = inc
        deps = set()
        stok = None
        if dma:
            stok = writes[0] if writes else reads[0]
            ins.stok = stok
        for t in reads:
            if t.w is not None:
                deps.add(t.w)
        for t in writes:
            if t.w is not None:
                if dma and t.w.dma and t.w.stok is stok:
                    pass
                elif (not dma) and (not t.w.dma) and t.w.eng == eng:
                    pass
                else:
                    deps.add(t.w)
            for re_, r in t.r.items():
                if dma or re_ != eng:
                    deps.add(r)
            for r in t.rd:
                deps.add(r)
        deps.discard(ins)
        ins.deps = deps
        if fn is not None:
            for t in reads:
                if dma:
                    t.rd.append(ins)
                else:
                    t.r[eng] = ins
        for t in writes:
            t.w = ins
            t.r = {}
            t.rd = []
        if dma:
            if stok.dsem is None:
                if self.free_sems:
                    stok.dsem, stok.dcount = self.free_sems.pop()
                else:
                    stok.dsem = self.new_sem("d" + stok.name)
                    stok.dcount = 0
                if self.alloc_es is not self.es:
                    self.phase_toks.append(stok)
            stok.dcount += inc
            ins.sig = (stok.dsem, stok.dcount)
            ins.signaled = True
            self.dmas.append(ins)
            self.recent_dmas.append(ins)
        self.streams[eng].append(ins)
        self.all.append(ins)
        return ins

    def dma(self, eng, out, in_, reads=(), writes=()):
        return self.op(eng, lambda e: e.dma_start(out=out, in_=in_), reads, writes, dma=True)

    def cc(self, out, in_, reads, writes):
        groups = [[0, 1], [2, 3], [4, 5], [6, 7]]
        return self.op("pool", lambda e: e.collective_compute("AllGather", ALU.bypass, replica_groups=groups, ins=[in_], outs=[out]),
                       reads, writes, dma=True, inc=1)

    def final_wait(self):
        fin = Ins("sp", None, False)
        fin.deps = set(self.dmas)
        self.streams["sp"].append(fin)
        self.all.append(fin)

    def emit(self):
        nc = self.nc
        for ins in self.all:
            for d in ins.deps:
                if d.eng == "pe" and ins.eng == "pe":
                    continue
                d.signaled = True
        for e in ENGS:
            cnt = 0
            sem = None
            for ins in self.streams[e]:
                if ins.dma or not ins.signaled or ins.fn is None:
                    continue
                if sem is None or cnt >= EPOCH:
                    sem = self.new_sem("e" + e)
                    cnt = 0
                cnt += 1
                ins.sig = (sem, cnt)
        nwaits = {e: 0 for e in ENGS}

        def run_stream(e, eng):
            known = {}
            for ins in self.streams[e]:
                need = {}
                for d in ins.deps:
                    if d.eng == "pe" and e == "pe":
                        continue
                    if d.sig is None:
                        continue
                    s, v = d.sig
                    k = id(s)
                    if known.get(k, 0) >= v:
                        continue
                    if k not in need or need[k][1] < v:
                        need[k] = (s, v)
                for k, (s, v) in need.items():
                    eng.wait_ge(s, v)
                    known[k] = v
                    nwaits[e] += 1
                    if self.log is not None:
                        self.log.append(f"{e}: wait {s.name if hasattr(s, 'name') else s} >= {v}")
                if ins.fn is None:
                    continue
                bi = ins.fn(eng)
                if ins.signaled:
                    s, v = ins.sig
                    bi.then_inc(s, ins.inc if ins.dma else 1)
                if self.log is not None:
                    self.log.append(f"{e}: L{ins.fn.__code__.co_firstlineno} sig={(ins.sig[0].name if hasattr(ins.sig[0], 'name') else ins.sig[0], ins.sig[1]) if ins.signaled else None}")

        with nc.Block() as block:
            @block.tensor
            def _(eng):
                run_stream("pe", eng)

            @block.scalar
            def _(eng):
                run_stream("act", eng)

            @block.vector
            def _(eng):
                run_stream("dve", eng)

            @block.gpsimd
            def _(eng):
                run_stream("pool", eng)

            @block.sync
            def _(eng):
                run_stream("sp", eng)
        self.nwaits = nwaits
        self.ninstr = {e: len(self.streams[e]) for e in ENGS}


class Banks:
    def __init__(self, P, n, prefix="bk", aps=None, toks=None):
        if aps is not None:
            self.aps, self.toks, n = aps, toks, len(aps)
        else:
            self.aps = [P.psum(f"{prefix}{i}", [128, 512]) for i in range(n)]
            self.toks = [Tok(f"{prefix}{i}") for i in range(n)]
        self.i = 0
        self.n = n

    def next(self):
        i = self.i
        self.i = (self.i + 1) % self.n
        return self.aps[i], self.toks[i]


class Ring:
    def __init__(self, P, name, shape, dt, n):
        self.aps = [P.sbuf(f"{name}{i}", shape, dt) for i in range(n)]
        self.toks = [Tok(f"{name}{i}") for i in range(n)]
        self.i = 0
        self.n = n

    def next(self):
        i = self.i
        self.i = (self.i + 1) % self.n
        return self.aps[i], self.toks[i]


def rmsnorm_fm(P, banks, x_sb, xtok, nw_sb, nw_tok, ones_sb, ones_tok, hT, htok, sq_ring, rstd_ring, ntt=4):
    for tt in range(ntt):
        sl = slice(tt * 512, (tt + 1) * 512)
        sq, sqt = sq_ring.next()
        P.op("act", lambda e, sq=sq, sl=sl: e.activation(out=sq[:], in_=x_sb[:, :, sl], func=AF.Square),
             [xtok[c][tt] for c in range(NCH)], [sqt])
        bk, bkt = banks.next()
        for c in range(NCH):
            P.op("pe", lambda e, bk=bk, sq=sq, c=c: e.matmul(bk[:], lhsT=ones_sb[:], rhs=sq[:, c, :], start=(c == 0), stop=(c == NCH - 1)),
                 [ones_tok, sqt], [bkt])
        rs, rst = rstd_ring.next()
        P.op("act", lambda e, rs=rs, bk=bk: e.activation(out=rs[:], in_=bk[:], func=AF.Ln, bias=EPS, scale=1.0), [bkt], [rst])
        P.op("act", lambda e, rs=rs: e.activation(out=rs[:], in_=rs[:], func=AF.Exp, scale=-0.5), [rst], [rst])
        for c in range(NCH):
            P.op("dve", lambda e, c=c, sl=sl, rs=rs: e.scalar_tensor_tensor(
                out=hT[:, c, sl], in0=x_sb[:, c, sl], scalar=nw_sb[:, c:c + 1], in1=rs[:], op0=ALU.mult, op1=ALU.mult),
                [xtok[c][tt], nw_tok, rst], [htok[tt]])


def ffn_body(P, banks, hbanks, x_sb, xtok, xh_sb, xh_tok, dr, dbg=None):
    nc = P.nc
    nw_sb = P.sbuf("f_nw", [128, NCH], F32); nw_tok = Tok("f_nw")
    cw_sb = P.sbuf("f_cw", [128, NFT * 4], F32); cw_tok = Tok("f_cw")
    ones_sb = P.sbuf("f_ones", [128, 128], BF16); ones_tok = Tok("f_ones")
    hT = P.sbuf("f_hT", [128, NCH, TC], BF16); htok = [Tok(f"f_h{t}") for t in range(4)]
    hcat = [P.sbuf(f"f_hcat{i}", [128, NCH, 2], BF16) for i in range(2)]
    hcat_tok = [Tok(f"f_hcat{i}") for i in range(2)]
    sq_ring = Ring(P, "f_sq", [128, NCH, 512], BF16, 1)
    rstd_ring = Ring(P, "f_rstd", [128, 512], F32, 2)
    P.dma("sp", nw_sb[:], dr["nw"], writes=[nw_tok])
    P.dma("sp", cw_sb[:], dr["cw"], writes=[cw_tok])
    P.op("dve", lambda e: e.memset(ones_sb[:], 1.0 / D), [], [ones_tok])

    rmsnorm_fm(P, banks, x_sb, xtok, nw_sb, nw_tok, ones_sb, ones_tok, hT, htok, sq_ring, rstd_ring)

    sqh = P.sbuf("f_sqh", [128, NCH, 2], BF16); sqh_tok = Tok("f_sqh")
    rsh = P.sbuf("f_rsh", [128, 2], F32); rsh_tok = Tok("f_rsh")
    hext = P.sbuf("f_hext", [128, NCH, 2], BF16); hext_tok = Tok("f_hext")
    P.op("act", lambda e: e.activation(out=sqh[:], in_=xh_sb[:], func=AF.Square), [xh_tok], [sqh_tok])
    bk, bkt = banks.next()
    for c in range(NCH):
        P.op("pe", lambda e, c=c, bk=bk: e.matmul(bk[:, 0:2], lhsT=ones_sb[:], rhs=sqh[:, c, :], start=(c == 0), stop=(c == NCH - 1)),
             [ones_tok, sqh_tok], [bkt])
    P.op("act", lambda e, bk=bk: e.activation(out=rsh[:], in_=bk[:, 0:2], func=AF.Sqrt, bias=EPS, scale=1.0), [bkt], [rsh_tok])
    P.op("dve", lambda e: e.reciprocal(out=rsh[:], in_=rsh[:]), [rsh_tok], [rsh_tok])
    for c in range(NCH):
        P.op("dve", lambda e, c=c: e.scalar_tensor_tensor(out=hext[:, c, :], in0=xh_sb[:, c, :], scalar=nw_sb[:, c:c + 1], in1=rsh[:],
                                                           op0=ALU.mult, op1=ALU.mult), [xh_tok, nw_tok, rsh_tok], [hext_tok])
    P.op("dve", lambda e: e.tensor_copy(out=hcat[0][:, :, 0:1], in_=hext[:, :, 0:1]), [hext_tok], [hcat_tok[0]])
    P.op("dve", lambda e: e.tensor_copy(out=hcat[0][:, :, 1:2], in_=hT[:, :, 1024:1025]), [htok[2]], [hcat_tok[0]])
    P.op("dve", lambda e: e.tensor_copy(out=hcat[1][:, :, 0:1], in_=hT[:, :, 1023:1024]), [htok[1]], [hcat_tok[1]])
    P.op("dve", lambda e: e.tensor_copy(out=hcat[1][:, :, 1:2], in_=hext[:, :, 1:2]), [hext_tok], [hcat_tok[1]])

    if dbg is not None:
        P.dma("sp", dbg["hT"].rearrange("(c p) t -> p c t", p=128), hT[:], reads=htok)
    GSZ = 11
    ws_ring = Ring(P, "f_ws", [128, 1024], F32, 2)
    wb_ring = Ring(P, "f_wb", [128, NCH, 128], BF16, 5)
    wds_ring = Ring(P, "f_wds", [128, 1024], F32, 1)
    wdb = P.sbuf("f_wdb", [128, GSZ, 1024], BF16); wdb_tok = [Tok(f"f_wdb{j}") for j in range(GSZ)]
    act = P.sbuf("f_act", [128, GSZ, 1024], BF16); act_tok = [Tok(f"f_act{j}") for j in range(GSZ)]
    u_ring = Ring(P, "f_u", [128, 1026], F32, 2)
    c_ring = Ring(P, "f_c", [128, 1024], F32, 5)
    tmp_ring = Ring.__new__(Ring)
    sqv = sq_ring.aps[0][:].rearrange("p c t -> p (c t)").bitcast(F32)
    tmp_ring.aps = [sqv[:, 0:1024], sqv[:, 1024:2048]]
    tmp_ring.toks = [Tok("f_tmp0"), Tok("f_tmp1")]
    tmp_ring.i = 0
    tmp_ring.n = 2

    pending = [None]
    PF = 3
    tiles = [(hh_, g_, jj_, kind_) for hh_ in range(2) for g_ in range(2) for jj_ in range(GSZ) for kind_ in range(2)]
    prepped = {}

    def prep(idx):
        if idx >= len(tiles) or idx in prepped:
            return
        hh_, g_, jj_, kind_ = tiles[idx]
        ft_ = 2 * (g_ * GSZ + jj_) + kind_
        ws, wst = ws_ring.next()
        P.dma("sp", ws[:], dr["wup"][ft_], writes=[wst])
        wb, wbt = wb_ring.next()
        P.op("dve", lambda e, ws=ws, wb=wb: e.tensor_copy(out=wb[:].rearrange("p c f -> p (c f)"), in_=ws[:]), [wst], [wbt])
        prepped[idx] = (wb, wbt)

    for i_ in range(PF):
        prep(i_)
    tidx = -1
    for hh in range(2):
        t0 = hh * 1024
        for g in range(2):
            for jj in range(GSZ):
                ws, wst = wds_ring.next()
                P.dma("sp", ws[:], dr["wdn"][g * GSZ + jj], writes=[wst])
                P.op("dve", lambda e, ws=ws, jj=jj: e.tensor_copy(out=wdb[:, jj, :], in_=ws[:]), [wst], [wdb_tok[jj]])
                cgate = None
                for kind in range(2):
                    ft = 2 * (g * GSZ + jj) + kind
                    tidx += 1
                    prep(tidx + PF)
                    wb, wbt = prepped.pop(tidx)
                    b0, b0t = banks.next()
                    b1, b1t = banks.next()
                    hcol = 2 * ((hh * NFT + ft) // 2)
                    hbank, hslot_tok = hbanks.next()
                    for k in range(NCH):
                        P.op("pe", lambda e, wb=wb, k=k, b0=b0, t0=t0: e.matmul(b0[:], lhsT=wb[:, k, :], rhs=hT[:, k, t0:t0 + 512], start=(k == 0), stop=(k == NCH - 1)),
                             [wbt, htok[2 * hh]], [b0t])
                        P.op("pe", lambda e, wb=wb, k=k, b1=b1, t0=t0: e.matmul(b1[:], lhsT=wb[:, k, :], rhs=hT[:, k, t0 + 512:t0 + 1024], start=(k == 0), stop=(k == NCH - 1)),
                             [wbt, htok[2 * hh + 1]], [b1t])
                        P.op("pe", lambda e, wb=wb, k=k, hcol=hcol, hh=hh, hbank=hbank: e.matmul(hbank[:, hcol:hcol + 2], lhsT=wb[:, k, :], rhs=hcat[hh][:, k, :], start=(k == 0), stop=(k == NCH - 1)),
                             [wbt, hcat_tok[hh]], [hslot_tok])
                    u, ut = u_ring.next()
                    P.op("act", lambda e, u=u, b0=b0: e.copy(out=u[:, 1:513], in_=b0[:]), [b0t], [ut])
                    P.op("act", lambda e, u=u, b1=b1: e.copy(out=u[:, 513:1025], in_=b1[:]), [b1t], [ut])
                    P.op("act", lambda e, u=u, hcol=hcol, hbank=hbank: e.copy(out=u[:, 0:1026:1025], in_=hbank[:, hcol:hcol + 2]), [hslot_tok], [ut])
                    c, ct = c_ring.next()
                    o = ft * 4
                    P.op("act", lambda e, c=c, u=u, o=o: e.activation(out=c[:], in_=u[:, 1:1025], func=AF.Identity, scale=cw_sb[:, o + 1:o + 2], bias=cw_sb[:, o + 3:o + 4]),
                         [ut, cw_tok], [ct])
                    tm, tmt = tmp_ring.next()
                    P.op("act", lambda e, tm=tm, u=u, o=o: e.activation(out=tm[:], in_=u[:, 2:1026], func=AF.Identity, scale=cw_sb[:, o + 2:o + 3]), [ut, cw_tok], [tmt])
                    if pending[0] is not None:
                        pending[0]()
                        pending[0] = None
                    P.op("dve", lambda e, c=c, u=u, o=o: e.scalar_tensor_tensor(out=c[:], in0=u[:, 0:1024], scalar=cw_sb[:, o:o + 1], in1=c[:],
                                                                               op0=ALU.mult, op1=ALU.add), [ut, cw_tok, ct], [ct])
                    P.op("pool", lambda e, c=c, tm=tm: e.tensor_tensor(out=c[:], in0=c[:], in1=tm[:], op=ALU.add), [tmt, ct], [ct])
                    if kind == 0:
                        def fin(c=c, ct=ct):
                            P.op("act", lambda e, c=c: e.activation(out=c[:], in_=c[:], func=AF.Silu), [ct], [ct])
                        cgate = (c, ct)
                    else:
                        def fin(c=c, ct=ct, cgate=cgate, jj=jj):
                            cg, cgt = cgate
                            P.op("pool", lambda e, c=c, cg=cg, jj=jj: e.tensor_tensor(out=act[:, jj, :], in0=c[:], in1=cg[:], op=ALU.mult),
                                 [ct, cgt], [act_tok[jj]])
                    pending[0] = fin
            if pending[0] is not None:
                pending[0]()
                pending[0] = None
            for d in range(NCH):
                for t2 in range(2):
                    bk, bkt = banks.next()
                    for jj in range(GSZ):
                        P.op("pe", lambda e, bk=bk, jj=jj, d=d, t2=t2: e.matmul(bk[:], lhsT=wdb[:, jj, d * 128:(d + 1) * 128], rhs=act[:, jj, t2 * 512:(t2 + 1) * 512],
                                                                                start=(jj == 0), stop=(jj == GSZ - 1)),
                             [wdb_tok[jj], act_tok[jj]], [bkt])
                    tt = hh * 2 + t2
                    sl = slice(tt * 512, (tt + 1) * 512)
                    P.op("dve", lambda e, bk=bk, d=d, sl=sl: e.tensor_tensor(out=x_sb[:, d, sl], in0=x_sb[:, d, sl], in1=bk[:], op=ALU.add),
                         [bkt, xtok[d][tt]], [xtok[d][tt]])


def build_ffn(debug=False):
    nc = bass.Bass("TRN2", target_bir_lowering=False)
    dbg = None
    if debug:
        dbg = {"hT": nc.dram_tensor("dbg_hT", [D, TC], BF16, kind="ExternalOutput").ap(),
               "act": nc.dram_tensor("dbg_act", [128, 11 * 1024], BF16, kind="ExternalOutput").ap(),
               "u": nc.dram_tensor("dbg_u", [128, 1026], F32, kind="ExternalOutput").ap(),
               "c": nc.dram_tensor("dbg_c", [128, 1024], F32, kind="ExternalOutput").ap()}
    xT = nc.dram_tensor("xT", [D, TC], F32, kind="ExternalInput").ap()
    xh = nc.dram_tensor("xh", [D, 2], F32, kind="ExternalInput").ap()
    nw = nc.dram_tensor("nw", [128, NCH], F32, kind="ExternalInput").ap()
    wup = nc.dram_tensor("wup", [NFT, 128, 1024], F32, kind="ExternalInput").ap()
    cw = nc.dram_tensor("cw", [128, NFT * 4], F32, kind="ExternalInput").ap()
    wdn = nc.dram_tensor("wdn", [NPAIR, 128, 1024], F32, kind="ExternalInput").ap()
    yT = nc.dram_tensor("yT", [D, TC], F32, kind="ExternalOutput").ap()
    with contextlib.ExitStack() as es:
        P = Prog(nc, es)
        banks = Banks(P, 6)
        hbanks = Banks(P, 2, "hb")
        x_sb = P.sbuf("x", [128, NCH, TC], F32)
        xtok = [[Tok(f"x{c}_{t}") for t in range(4)] for c in range(NCH)]
        xh_sb = P.sbuf("xh", [128, NCH, 2], F32); xh_tok = Tok("xh")
        xv = xT.rearrange("(c p) t -> p c t", p=128)
        for tt in range(4):
            sl = slice(tt * 512, (tt + 1) * 512)
            P.dma("sp", x_sb[:, :, sl], xv[:, :, sl], writes=[xtok[c][tt] for c in range(NCH)])
        P.dma("sp", xh_sb[:], xh.rearrange("(c p) t -> p c t", p=128), writes=[xh_tok])
        dr = {"nw": nw[:, :], "cw": cw[:, :], "wup": [wup[i] for i in range(NFT)], "wdn": [wdn[i] for i in range(NPAIR)]}
        ffn_body(P, banks, hbanks, x_sb, xtok, xh_sb, xh_tok, dr, dbg)
        yv = yT.rearrange("(c p) t -> p c t", p=128)
        for tt in range(4):
            sl = slice(tt * 512, (tt + 1) * 512)
            P.dma("sp", yv[:, :, sl], x_sb[:, :, sl], reads=[xtok[c][tt] for c in range(NCH)])
        P.final_wait()
        P.emit()
        print("ffn program:", P.ninstr, "waits", P.nwaits, "sems", P.nsem)
    return nc


def fm(v):
    return np.ascontiguousarray(v.reshape(NCH, 128).T)


def ffn_weights_layout(w_up, w_conv, b_conv, w_down):
    order = []
    for j in range(NPAIR):
        order.append(NPAIR + j)
        order.append(j)
    wup = np.empty((NFT, 128, NCH, 128), np.float32)
    cw = np.empty((128, NFT, 4), np.float32)
    for i, ft in enumerate(order):
        cols = slice(ft * 128, (ft + 1) * 128)
        wup[i] = w_up[:, cols].reshape(NCH, 128, 128).transpose(1, 0, 2)
        cw[:, i, 0:3] = w_conv[:, cols].T
        cw[:, i, 3] = b_conv[cols]
    wdn = np.ascontiguousarray(w_down.reshape(NPAIR, 128, D))
    return wup.reshape(NFT, 128, 1024), np.ascontiguousarray(cw.reshape(128, NFT * 4)), wdn


def halos(xT_cores):
    out = []
    for c in range(NCORES):
        h = np.zeros((D, 2), np.float32)
        if c % 2 == 1:
            h[:, 0] = xT_cores[c - 1][:, -1]
        else:
            h[:, 1] = xT_cores[c + 1][:, 0]
        out.append(h)
    return out


_PROGS = {}


def get_prog(name, builder):
    if name not in _PROGS:
        _PROGS[name] = builder()
    return _PROGS[name]


def run_ffn(xT_cores, nw, w_up, w_conv, b_conv, w_down, debug=False):
    nc = get_prog("ffn" + str(debug), lambda: build_ffn(debug))
    wup, cw, wdn = ffn_weights_layout(w_up, w_conv, b_conv, w_down)
    hl = halos(xT_cores)
    nwl = fm(nw)
    in_maps = [{"xT": xT_cores[c], "xh": hl[c], "nw": nwl, "wup": wup, "cw": cw, "wdn": wdn} for c in range(NCORES)]
    res = run_bass_kernel_spmd(nc, in_maps, core_ids=list(range(NCORES)))
    if debug:
        return [np.asarray(r["yT"]) for r in res.results], res.results
    return [np.asarray(r["yT"]) for r in res.results]


def load_x(P, xT, x_sb, xtok):
    xv = xT.rearrange("(c p) t -> p c t", p=128)
    for tt in range(4):
        sl = slice(tt * 512, (tt + 1) * 512)
        P.dma("sp", x_sb[:, :, sl], xv[:, :, sl], writes=[xtok[c][tt] for c in range(NCH)])


def store_x(P, yT, x_sb, xtok):
    yv = yT.rearrange("(c p) t -> p c t", p=128)
    for tt in range(4):
        sl = slice(tt * 512, (tt + 1) * 512)
        P.dma("sp", yv[:, :, sl], x_sb[:, :, sl], reads=[xtok[c][tt] for c in range(NCH)])


def load_cast(P, dram_ap, dst_ap, dst_tok, stage_ring, eng="pool"):
    ws, wst = stage_ring.next()
    n = dst_ap.shape[-1] if len(dst_ap.shape) == 2 else None
    P.dma("sp", ws[:, 0:dram_ap.shape[-1]], dram_ap, writes=[wst])
    if eng == "act":
        P.op(eng, lambda e: e.copy(out=dst_ap, in_=ws[:, 0:dram_ap.shape[-1]]), [wst], [dst_tok])
    else:
        P.op(eng, lambda e: e.tensor_copy(out=dst_ap, in_=ws[:, 0:dram_ap.shape[-1]]), [wst], [dst_tok])


def rope_gen(P, cs_sb, cs_tok, msk_sb, msk_tok):
    I32 = mybir.dt.int32
    TWO_PI = 2.0 * math.pi
    rowt = P.sbuf("rowt", [128, TC], F32); rowt_tok = Tok("rowt")
    colt = P.sbuf("colt", [128, TC], F32); colt_tok = Tok("colt")
    kint = P.sbuf("kint", [128, TC], I32); kint_tok = Tok("kint")
    smi = P.sbuf("ropesmi", [128, 4], I32); smi_tok = Tok("ropesmi")
    sm = P.sbuf("ropesm", [128, 8], F32); sm_tok = Tok("ropesm")
    P.op("pool", lambda e: e.iota(rowt[:], [[1, 32], [0, 64]], base=0, channel_multiplier=0, allow_small_or_imprecise_dtypes=True), [], [rowt_tok])
    P.op("pool", lambda e: e.iota(colt[:], [[0, 32], [1, 64]], base=0, channel_multiplier=0, allow_small_or_imprecise_dtypes=True), [], [colt_tok])
    P.op("pool", lambda e: e.iota(smi[:, 0:1], [[0, 1]], base=0, channel_multiplier=1), [], [smi_tok])
    P.op("dve", lambda e: e.tensor_single_scalar(out=smi[:, 1:2], in_=smi[:, 0:1], scalar=31, op=ALU.bitwise_and), [smi_tok], [smi_tok])
    P.op("dve", lambda e: e.tensor_single_scalar(out=smi[:, 2:3], in_=smi[:, 0:1], scalar=5, op=ALU.arith_shift_right), [smi_tok], [smi_tok])
    P.op("dve", lambda e: e.tensor_single_scalar(out=smi[:, 3:4], in_=smi[:, 2:3], scalar=1, op=ALU.bitwise_and), [smi_tok], [smi_tok])
    P.op("dve", lambda e: e.tensor_copy(out=sm[:, 1:2], in_=smi[:, 1:2]), [smi_tok], [sm_tok])
    P.op("dve", lambda e: e.tensor_copy(out=sm[:, 4:5], in_=smi[:, 3:4]), [smi_tok], [sm_tok])
    P.op("act", lambda e: e.activation(out=sm[:, 2:3], in_=sm[:, 1:2], func=AF.Exp, scale=-math.log(10000.0) / 32.0), [sm_tok], [sm_tok])
    P.op("dve", lambda e: e.tensor_scalar(out=sm[:, 5:6], in0=sm[:, 4:5], scalar1=-32.0, scalar2=32.0, op0=ALU.mult, op1=ALU.add), [sm_tok], [sm_tok])
    P.op("dve", lambda e: e.tensor_tensor(out=sm[:, 5:6], in0=sm[:, 5:6], in1=msk_sb[:, 1:2], op=ALU.mult), [sm_tok, msk_tok], [sm_tok])
    P.op("dve", lambda e: e.tensor_tensor(out=sm[:, 6:7], in0=sm[:, 5:6], in1=sm[:, 2:3], op=ALU.mult), [sm_tok], [sm_tok])
    P.op("dve", lambda e: e.tensor_tensor(out=colt[:], in0=colt[:], in1=rowt[:], op=ALU.subtract), [colt_tok, rowt_tok], [colt_tok])
    P.op("dve", lambda e: e.scalar_tensor_tensor(out=rowt[:], in0=colt[:], scalar=sm[:, 4:5], in1=rowt[:], op0=ALU.mult, op1=ALU.add), [colt_tok, rowt_tok, sm_tok], [rowt_tok])
    P.op("act", lambda e: e.activation(out=rowt[:], in_=rowt[:], func=AF.Identity, scale=sm[:, 2:3], bias=sm[:, 6:7]), [rowt_tok, sm_tok], [rowt_tok])
    for shift, lo in ((0.0, TC), (math.pi / 2.0, 0)):
        P.op("dve", lambda e, shift=shift: e.tensor_scalar(out=kint[:], in0=rowt[:], scalar1=shift, scalar2=1.0 / TWO_PI, op0=ALU.add, op1=ALU.mult), [rowt_tok], [kint_tok])
        P.op("dve", lambda e: e.tensor_copy(out=colt[:], in_=kint[:]), [kint_tok], [colt_tok])
        P.op("dve", lambda e: e.scalar_tensor_tensor(out=colt[:], in0=colt[:], scalar=-TWO_PI, in1=rowt[:], op0=ALU.mult, op1=ALU.add), [colt_tok, rowt_tok], [colt_tok])
        P.op("act", lambda e, shift=shift, lo=lo: e.activation(out=cs_sb[:, lo:lo + TC], in_=colt[:], func=AF.Sin, scale=1.0 - 2e-6, bias=shift * (1.0 - 2e-6)), [colt_tok], [cs_tok])


def attn1_body(P, banks, x_sb, xtok, dr):
    nw, wqk, wv, gains, cs, rot = dr["nw"], dr["wqk"], dr["wv"], dr["gains"], dr["cs"], dr["rot"]
    if True:
        nw_sb = P.sbuf("nw", [128, NCH], F32); nw_tok = Tok("nw")
        g_sb = P.sbuf("g", [128, 2], F32); g_tok = Tok("g")
        cs_sb = P.sbuf("cs", [128, 2 * TC], F32); cs_tok = Tok("cs")
        rot32 = P.sbuf("rot32", [128, 128], F32); rot32_tok = Tok("rot32")
        rotb = P.sbuf("rotb", [128, 128], BF16); rotb_tok = Tok("rotb")
        onesD = P.sbuf("onesD", [128, 128], BF16); onesD_tok = Tok("onesD")
        onesH = P.sbuf("onesH", [128, 128], BF16); onesH_tok = Tok("onesH")
        P.dma("sp", nw_sb[:], nw, writes=[nw_tok])
        P.dma("sp", g_sb[:], gains, writes=[g_tok])
        P.dma("sp", cs_sb[:], cs, reads=([dr["cs_dtok"]] if dr.get("cs_dtok") is not None else []), writes=[cs_tok])
        P.dma("sp", rot32[:], rot, writes=[rot32_tok])
        P.op("dve", lambda e: e.tensor_copy(out=rotb[:], in_=rot32[:]), [rot32_tok], [rotb_tok])
        P.op("dve", lambda e: e.memset(onesD[:], 1.0 / D), [], [onesD_tok])
        P.op("dve", lambda e: e.memset(onesH[:], 1.0 / 128), [], [onesH_tok])
        hT = P.sbuf("hT", [128, NCH, TC], BF16); htok = [Tok(f"h{t}") for t in range(4)]
        sq_ring = Ring(P, "sq", [128, NCH, 512], BF16, 1)
        rstd_ring = Ring(P, "rstd", [128, 512], F32, 2)
        rmsnorm_fm(P, banks, x_sb, xtok, nw_sb, nw_tok, onesD, onesD_tok, hT, htok, sq_ring, rstd_ring)

        ws_ring = Ring(P, "ws", [128, 2048], F32, 2)
        wb_ring = Ring(P, "wb", [128, NCH, 128], BF16, 3)
        sqh_ring = Ring(P, "sqh", [128, 512], BF16, 3)
        rs_ring = Ring(P, "rs", [128, 512], F32, 3)
        qn_ring = Ring(P, "qn", [128, 512], F32, 4)
        qnb_ring = Ring(P, "qnb", [128, 512], BF16, 4)
        t1_ring = Ring(P, "t1", [128, 512], F32, 2)
        t2_ring = Ring(P, "t2", [128, 512], F32, 2)
        qo_ring = Ring(P, "qo", [128, TC], BF16, 3)
        wvb = P.sbuf("wvb", [128, NCH, 256], BF16); wvb_tok = Tok("wvb")
        load_cast(P, wv, wvb[:].rearrange("p c f -> p (c f)"), wvb_tok, ws_ring)
        v_sb = P.sbuf("v", [128, 16, 256], BF16); v_tok = Tok("v")
        for i in range(16):
            bk, bkt = banks.next()
            for k in range(NCH):
                P.op("pe", lambda e, bk=bk, i=i, k=k: e.matmul(bk[:, 0:256], lhsT=hT[:, k, i * 128:(i + 1) * 128], rhs=wvb[:, k, :], start=(k == 0), stop=(k == NCH - 1)),
                     [wvb_tok, htok[i // 4]], [bkt])
            P.op("act", lambda e, bk=bk, i=i: e.copy(out=v_sb[:, i, :], in_=bk[:, 0:256]), [bkt], [v_tok])
        P.dma("sp", dr["v_out"].rearrange("(i p) f -> p i f", p=128), v_sb[:], reads=[v_tok], writes=[dr["kv_dtok"]])
        forder = [8, 9, 0, 1, 2, 3, 4, 5, 6, 7]
        items = [(ft, tt) for ft in forder for tt in range(4)]
        st = {}
        wbs = {}
        qos = {}

        def stage_a(i):
            ft, tt = items[i]
            if tt == 0:
                pos_ = forder.index(ft)
                for f_ in forder[pos_:pos_ + 2]:
                    if f_ not in wbs:
                        wb, wbt = wb_ring.next()
                        load_cast(P, wqk[f_], wb[:].rearrange("p c f -> p (c f)"), wbt, ws_ring, eng="dve")
                        wbs[f_] = (wb, wbt)
                qos[ft] = qo_ring.next()
            wb, wbt = wbs[ft]
            sl = slice(tt * 512, (tt + 1) * 512)
            bk, bkt = banks.next()
            for k in range(NCH):
                P.op("pe", lambda e, bk=bk, wb=wb, k=k, sl=sl: e.matmul(bk[:], lhsT=wb[:, k, :], rhs=hT[:, k, sl], start=(k == 0), stop=(k == NCH - 1)),
                     [wbt, htok[tt]], [bkt])
            sq, sqt = sqh_ring.next()
            P.op("act", lambda e, sq=sq, bk=bk: e.activation(out=sq[:], in_=bk[:], func=AF.Square), [bkt], [sqt])
            st[i] = {"bk": bk, "bkt": bkt, "sq": sq, "sqt": sqt}

        def stage_b(i):
            ft, tt = items[i]
            d_ = st[i]
            gcol = 0 if ft < 8 else 1
            b2, b2t = banks.next()
            P.op("pe", lambda e, b2=b2, sq=d_["sq"]: e.matmul(b2[:], lhsT=onesH[:], rhs=sq[:], start=True, stop=True), [onesH_tok, d_["sqt"]], [b2t])
            rs, rst = rs_ring.next()
            P.op("act", lambda e, rs=rs, b2=b2: e.activation(out=rs[:], in_=b2[:], func=AF.Ln, bias=EPS, scale=1.0), [b2t], [rst])
            P.op("act", lambda e, rs=rs: e.activation(out=rs[:], in_=rs[:], func=AF.Exp, scale=-0.5), [rst], [rst])
            qn, qnt = qn_ring.next()
            P.op("dve", lambda e, qn=qn, bk=d_["bk"], rs=rs, gcol=gcol: e.scalar_tensor_tensor(out=qn[:], in0=bk[:], scalar=g_sb[:, gcol:gcol + 1], in1=rs[:],
                                                                                         op0=ALU.mult, op1=ALU.mult), [d_["bkt"], g_tok, rst], [qnt])
            qnb, qnbt = qnb_ring.next()
            P.op("act", lambda e, qnb=qnb, qn=qn: e.copy(out=qnb[:], in_=qn[:]), [qnt], [qnbt])
            d_.update({"qn": qn, "qnt": qnt, "qnb": qnb, "qnbt": qnbt})

        def stage_c(i):
            ft, tt = items[i]
            d_ = st.pop(i)
            sl = slice(tt * 512, (tt + 1) * 512)
            qo, qot = qos[ft]
            b3, b3t = banks.next()
            P.op("pe", lambda e, b3=b3, qnb=d_["qnb"]: e.matmul(b3[:], lhsT=rotb[:], rhs=qnb[:], start=True, stop=True), [rotb_tok, d_["qnbt"]], [b3t])
            t1, t1t = t1_ring.next()
            P.op("pool", lambda e, t1=t1, qn=d_["qn"], sl=sl: e.tensor_tensor(out=t1[:], in0=qn[:], in1=cs_sb[:, sl], op=ALU.mult), [d_["qnt"], cs_tok], [t1t])
            t2, t2t = t2_ring.next()
            P.op("dve", lambda e, t2=t2, b3=b3, tt=tt: e.tensor_tensor(out=t2[:], in0=b3[:], in1=cs_sb[:, TC + tt * 512:TC + (tt + 1) * 512], op=ALU.mult),
                 [b3t, cs_tok], [t2t])
            P.op("dve", lambda e, qo=qo, t1=t1, t2=t2, sl=sl: e.tensor_tensor(out=qo[:, sl], in0=t1[:], in1=t2[:], op=ALU.add), [t1t, t2t], [qot])
            if tt == 3:
                dst = dr["q_out"][ft] if ft < 8 else dr["k_out"][ft - 8]
                P.dma("sp", dst, qo[:], reads=[qot], writes=[dr["q_dtok"] if ft < 8 else dr["kv_dtok"]])
                if ft == 9 and dr.get("after_kv") is not None:
                    dr["after_kv"]()

        n_items = len(items)
        for i in range(n_items + 3):
            if i < n_items:
                stage_a(i)
            if 0 <= i - 1 < n_items:
                stage_b(i - 1)
            if 0 <= i - 3 < n_items:
                stage_c(i - 3)


def attn2_body(P, sbanks, obanks, dbanks, x_sb, xtok, dr):
    SCALE = 128 ** -0.5
    if True:
        q_sb = P.sbuf("q", [128, 8, TC], BF16); q_tok = [Tok(f"q{h}") for h in range(8)]
        k_sb = P.sbuf("k", [128, 2, SEQ], BF16); k_tok = [Tok(f"k{h}") for h in range(2)]
        v_sb = P.sbuf("v", [128, 32, 256], BF16); v_tok = Tok("v")
        ones_b = P.sbuf("ones", [128, 128], BF16); ones_tok = Tok("ones")
        P.op("dve", lambda e: e.memset(ones_b[:], 1.0), [], [ones_tok])
        for h in range(8):
            P.dma("sp", q_sb[:, h, :], dr["q"][h], reads=[dr["q_dtok"]], writes=[q_tok[h]])
        for h in range(2):
            for r in range(2):
                P.dma("sp", k_sb[:, h, r * TC:(r + 1) * TC], dr["k"][h][r], reads=[dr["kvg_dtok"]], writes=[k_tok[h]])
        for r in range(2):
            P.dma("sp", v_sb[:, 16 * r:16 * r + 16, :], dr["v"][r].rearrange("(i p) f -> p i f", p=128), reads=[dr["kvg_dtok"]], writes=[v_tok])
        oT = P.sbuf("oT", [128, 8, TC], BF16); o_tok = [[Tok(f"o{h}_{t}") for t in range(4)] for h in range(8)]
        p_ring = Ring(P, "p", [128, 512], BF16, 3)
        rd_ring = Ring(P, "rd", [128, 512], F32, 2)
        ws_ring = Ring(P, "ws", [128, 1024], F32, 2)
        wob = P.sbuf("wob", [128, NCH, D], BF16); wob_tok = [Tok(f"wob{c}") for c in range(NCH)]
        for c in range(NCH):
            load_cast(P, dr["wo"][c], wob[:, c, :], wob_tok[c], ws_ring, eng="dve")
        steps = [(h, qb, kt) for h in range(8) for qb in range(4) for kt in range(32)]
        sc = {}

        def issue_s(i):
            h, qb, kt = steps[i]
            bk, bkt = sbanks.next()
            P.op("pe", lambda e, bk=bk, h=h, qb=qb, kt=kt: e.matmul(bk[:], lhsT=k_sb[:, h // 4, kt * 128:(kt + 1) * 128], rhs=q_sb[:, h, qb * 512:(qb + 1) * 512],
                                                                  start=True, stop=True), [k_tok[h // 4], q_tok[h]], [bkt])
            sc[i] = (bk, bkt)

        issue_s(0)
        issue_s(1)
        ob = db = None
        for i, (h, qb, kt) in enumerate(steps):
            if i + 2 < len(steps):
                issue_s(i + 2)
            bk, bkt = sc.pop(i)
            p, pt = p_ring.next()
            P.op("act", lambda e, p=p, bk=bk: e.activation(out=p[:], in_=bk[:], func=AF.Exp, scale=SCALE), [bkt], [pt])
            if kt == 0:
                ob = obanks.next()
                db = dbanks.next()
            P.op("pe", lambda e, ob=ob, p=p, h=h, kt=kt: e.matmul(ob[0][:], lhsT=v_sb[:, kt, (h // 4) * 128:(h // 4 + 1) * 128], rhs=p[:], start=(kt == 0), stop=(kt == 31)),
                 [v_tok, pt], [ob[1]])
            P.op("pe", lambda e, db=db, p=p, kt=kt: e.matmul(db[0][:], lhsT=ones_b[:], rhs=p[:], start=(kt == 0), stop=(kt == 31)),
                 [ones_tok, pt], [db[1]])
            if kt == 31:
                rd, rdt = rd_ring.next()
                P.op("dve", lambda e, rd=rd, db=db: e.reciprocal(out=rd[:], in_=db[0][:]), [db[1]], [rdt])
                P.op("dve", lambda e, rd=rd, ob=ob, h=h, qb=qb: e.tensor_tensor(out=oT[:, h, qb * 512:(qb + 1) * 512], in0=ob[0][:], in1=rd[:], op=ALU.mult),
                     [ob[1], rdt], [o_tok[h][qb]])
        for d in range(NCH):
            for tt in range(4):
                sl = slice(tt * 512, (tt + 1) * 512)
                bk, bkt = sbanks.next()
                for c in range(NCH):
                    P.op("pe", lambda e, bk=bk, c=c, d=d, sl=sl: e.matmul(bk[:], lhsT=wob[:, c, d * 128:(d + 1) * 128], rhs=oT[:, c, sl], start=(c == 0), stop=(c == NCH - 1)),
                         [wob_tok[c], o_tok[c][tt]], [bkt])
                P.op("dve", lambda e, bk=bk, d=d, sl=sl: e.tensor_tensor(out=x_sb[:, d, sl], in0=x_sb[:, d, sl], in1=bk[:], op=ALU.add),
                     [bkt, xtok[d][tt]], [xtok[d][tt]])


def rope_tables():
    pos = np.arange(SEQ)
    row = (pos // 64).astype(np.float32)
    col = (pos % 64).astype(np.float32)
    inv_freq = (np.float32(10000.0) ** (-np.arange(32, dtype=np.float32) / np.float32(32))).astype(np.float32)
    ang = np.concatenate([row[:, None] * inv_freq, col[:, None] * inv_freq], axis=-1).astype(np.float32)
    cos = np.cos(ang).astype(np.float32).T
    sin = np.sin(ang).astype(np.float32).T
    return np.concatenate([cos, cos], 0), np.concatenate([sin, sin], 0)


def rot_matrix():
    r = np.zeros((128, 128), np.float32)
    for m in range(64):
        r[m + 64, m] = -1.0
        r[m, m + 64] = 1.0
    return r


def tile_w(w, ncols_tile=128):
    n = w.shape[1] // ncols_tile
    return np.ascontiguousarray(w.reshape(NCH, 128, n, ncols_tile).transpose(2, 1, 0, 3).reshape(n, 128, NCH * ncols_tile))


def gla_body(P, banks, full, x_sb, xtok, msk_sb, msk_tok, dr):
    nw, wq, wk, wv, wg, wr, wgate, gn, consts = [dr[k_] for k_ in ("nw", "wq", "wk", "wv", "wg", "wr", "wgate", "gn", "consts")]
    QS = 128 ** -0.5
    if True:
        nw_sb = P.sbuf("nw", [128, NCH], F32); nw_tok = Tok("nw")
        cst = P.sbuf("cst", [128, 6 * 128], F32); cst_tok = Tok("cst")
        UF, UB, UFx, UBx, MF, MB = [cst[:, i * 128:(i + 1) * 128] for i in range(6)]
        gn_sb = P.sbuf("gn", [128, 2], F32); gn_tok = Tok("gn")
        wgate32 = P.sbuf("wgate32", [33, 1024], F32); wgate32_tok = Tok("wgate32")
        wgateb = P.sbuf("wgateb", [33, 1024], BF16); wgateb_tok = Tok("wgateb")
        sin_sb = P.sbuf("sin", [128, 8, 256], F32) if full else None
        sin_tok = Tok("sin")
        onesD = P.sbuf("onesD", [128, 128], BF16); onesD_tok = Tok("onesD")
        onesV = P.sbuf("onesV", [128, 128], BF16); onesV_tok = Tok("onesV")
        P.dma("sp", nw_sb[:], nw, writes=[nw_tok])
        P.dma("sp", cst[:], consts, writes=[cst_tok])
        P.dma("sp", gn_sb[:], gn, writes=[gn_tok])
        P.dma("sp", wgate32[:], wgate, writes=[wgate32_tok])
        P.op("dve", lambda e: e.tensor_copy(out=wgateb[:], in_=wgate32[:]), [wgate32_tok], [wgateb_tok])
        P.op("dve", lambda e: e.memset(onesD[:], 1.0 / D), [], [onesD_tok])
        P.op("dve", lambda e: e.memset(onesV[:], 1.0 / 256), [], [onesV_tok])
        hT = P.sbuf("hT", [128, NCH, TC], BF16); htok = [Tok(f"h{t}") for t in range(4)]
        sq_ring = Ring(P, "sq", [128, NCH, 512], BF16, 1)
        rstd_ring = Ring(P, "rstd", [128, 512], F32, 2)
        if full:
            xv = dr["x_spill"].rearrange("(c p) t -> p c t", p=128)
            for tt in range(4):
                sl = slice(tt * 512, (tt + 1) * 512)
                P.dma("sp", xv[:, :, sl], x_sb[:, :, sl], reads=[xtok[c][tt] for c in range(NCH)], writes=[dr["x_spill_dtok"]])
        rmsnorm_fm(P, banks, x_sb, xtok, nw_sb, nw_tok, onesD, onesD_tok, hT, htok, sq_ring, rstd_ring)
        if full:
            P.barrier()
            xa = x_sb[:].rearrange("p c t -> p (c t)").bitcast(BF16)
            sst = [xa[:, d_ * 8192:(d_ + 1) * 8192].rearrange("p (n v) -> p n v", n=32) for d_ in range(2)]
            qdec = [xa[:, 16384 + d_ * 2048:16384 + (d_ + 1) * 2048] for d_ in range(2)]
            kinv = [xa[:, 20480 + d_ * 2048:20480 + (d_ + 1) * 2048] for d_ in range(2)]
            kdec = [xa[:, 24576 + d_ * 2048:24576 + (d_ + 1) * 2048].rearrange("p (i d) -> p i d", i=16) for d_ in range(2)]
            v_sb = xa[:, 28672:32768].rearrange("p (i f) -> p i f", i=16)
        else:
            kdec = [P.sbuf(f"kdec{d_}", [128, 16, 128], BF16)[:] for d_ in range(2)]
            v_sb = P.sbuf("v", [128, 16, 256], BF16)[:]
        sst_tok = [[Tok(f"sst{d_}_{n}") for n in range(32)] for d_ in range(2)]
        ws2_ring = Ring(P, "ws2", [128, 2048], F32, 2)
        wrb = P.sbuf("wrb", [128, NCH, 32], BF16); wrb_tok = Tok("wrb")
        load_cast(P, wr, wrb[:].rearrange("p c f -> p (c f)"), wrb_tok, ws2_ring, eng="act")
        rT = P.sbuf("rT", [33, TC], BF16); rT_tok = Tok("rT")
        P.op("dve", lambda e: e.memset(rT[32:33, :], 1.0), [], [rT_tok])
        for tt in range(4):
            sl = slice(tt * 512, (tt + 1) * 512)
            bk, bkt = banks.next()
            for c in range(NCH):
                P.op("pe", lambda e, bk=bk, c=c, sl=sl: e.matmul(bk[0:32, :], lhsT=wrb[:, c, :], rhs=hT[:, c, sl], start=(c == 0), stop=(c == NCH - 1)), [wrb_tok, htok[tt]], [bkt])
            P.op("act", lambda e, bk=bk, sl=sl: e.copy(out=rT[0:32, sl], in_=bk[0:32, :]), [bkt], [rT_tok])

        wsets = []
        for i_ in range(2):
            wsets.append({"wqb": P.sbuf("wqb", [128, NCH, 128], BF16) if full else None, "wqb_tok": Tok("wqb"),
                          "wkb": P.sbuf("wkb", [128, NCH, 128], BF16), "wkb_tok": Tok("wkb"),
                          "wvb": P.sbuf("wvb", [128, NCH, 256], BF16), "wvb_tok": Tok("wvb"),
                          "wgb": P.sbuf("wgb", [128, NCH, 256], BF16) if full else None, "wgb_tok": Tok("wgb")})
        def load_head(h_):
            if h_ >= 4:
                return
            w_ = wsets[h_ % 2]
            if full:
                load_cast(P, wq[h_], w_["wqb"][:].rearrange("p c f -> p (c f)"), w_["wqb_tok"], ws2_ring, eng="act")
            load_cast(P, wk[h_], w_["wkb"][:].rearrange("p c f -> p (c f)"), w_["wkb_tok"], ws2_ring, eng="act")
            load_cast(P, wv[h_], w_["wvb"][:].rearrange("p c f -> p (c f)"), w_["wvb_tok"], ws2_ring, eng="act")
            if full:
                load_cast(P, wg[h_], w_["wgb"][:].rearrange("p c f -> p (c f)"), w_["wgb_tok"], ws2_ring, eng="act")

        load_head(0)
        kdec_tok = [[Tok(f"kdec{d_}_{t}") for t in range(4)] for d_ in range(2)]
        v_tok = [Tok(f"v{t}") for t in range(4)]
        dec = [P.sbuf(f"dec{d_}", [128, 32], F32) for d_ in range(2)]
        dec_tok = [[Tok(f"dec{d_}_{t}") for t in range(4)] for d_ in range(2)]
        if full:
            qk_tok = [[Tok(f"qk{d_}_{t}") for t in range(4)] for d_ in range(2)]
        S = [P.sbuf(f"S{d_}", [128, 256], F32) for d_ in range(2)]
        S_tok = [Tok(f"S{d_}") for d_ in range(2)]
        sqv_ = sq_ring.aps[0][:].rearrange("p c t -> p (c t)").bitcast(F32)
        ez_ring = Ring.__new__(Ring)
        ez_ring.aps = [sqv_[:, 0:1024].rearrange("p (a b) -> p a b", a=2)]; ez_ring.toks = [Tok("ez")]; ez_ring.i = 0; ez_ring.n = 1
        E_ring = Ring(P, "E", [128, 4, 512], F32, 1)
        EK_ring = Ring.__new__(Ring)
        EK_ring.aps = [sqv_[:, 1024:2048].rearrange("p (a b) -> p a b", a=2)]; EK_ring.toks = [Tok("EK")]; EK_ring.i = 0; EK_ring.n = 1
        if full:
            att_ring = Ring(P, "att", [128, 256], BF16, 4)
            o_ring = Ring(P, "oh", [128, 2, 512], F32, 1)
            sqo_ring = Ring(P, "sqo", [128, 2, 512], BF16, 1)
            rso_ring = Ring(P, "rso", [128, 512], F32, 1)
            sg_ring = Ring(P, "sg", [128, 2, 512], F32, 1)
            of_ring = Ring(P, "of", [128, 2, 512], BF16, 2)
        sout_sb = None
        if not full:
            sout_sb = P.sbuf("sout", [128, 8, 256], F32); sout_tok = Tok("sout")

        for h in range(4):
            if full:
                for dr_ in range(2):
                    P.dma("sp", kdec[dr_].rearrange("p a b -> p (a b)"), dr["kv_s"][h][:, dr_ * 2048:(dr_ + 1) * 2048], reads=[dr["kv_s_dtok"][h]], writes=kdec_tok[dr_])
                P.dma("sp", v_sb.rearrange("p a b -> p (a b)"), dr["kv_s"][h][:, 4096:8192], reads=[dr["kv_s_dtok"][h]], writes=v_tok)
                if h == 0:
                    P.dma("sp", sin_sb[:, 0:4, :].rearrange("p a b -> p (a b)"), dr["s_g"][0:128, 0:1024], reads=[dr["s_g_dtok"]], writes=[sin_tok])
                    P.dma("sp", sin_sb[:, 4:8, :].rearrange("p a b -> p (a b)"), dr["s_g"][128:256, 1024:2048], reads=[dr["s_g_dtok"]], writes=[sin_tok])
            w_ = wsets[h % 2]
            wqb, wqb_tok, wkb, wkb_tok = w_["wqb"], w_["wqb_tok"], w_["wkb"], w_["wkb_tok"]
            wvb, wvb_tok, wgb, wgb_tok = w_["wvb"], w_["wvb_tok"], w_["wgb"], w_["wgb_tok"]
            for tt in range(4):
                sl = slice(tt * 512, (tt + 1) * 512)
                zb = [banks.next() for _ in range(2)]
                for dr_ in range(2):
                    for i4 in range(4):
                        ti = tt * 4 + i4
                        P.op("pe", lambda e, dr_=dr_, i4=i4, ti=ti, zb=zb, h=h: e.matmul(
                            zb[dr_][0][:, i4 * 128:(i4 + 1) * 128], lhsT=rT[0:33, ti * 128:(ti + 1) * 128],
                            rhs=wgateb[0:33, dr_ * 512 + h * 128:dr_ * 512 + (h + 1) * 128], start=True, stop=True), [rT_tok, wgateb_tok], [zb[dr_][1]])
                ez, ezt = ez_ring.next()
                sp, spt = ez, ezt
                for dr_ in range(2):
                    P.op("act", lambda e, ez=ez, dr_=dr_, zb=zb: e.activation(out=ez[:, dr_, :], in_=zb[dr_][0][:], func=AF.Exp, scale=-1.0), [zb[dr_][1]], [ezt])
                P.op("act", lambda e, ez=ez, sp=sp: e.activation(out=sp[:], in_=ez[:], func=AF.Ln, bias=1.0, scale=1.0), [ezt], [spt])
                if not full:
                    kb = [banks.next() for _ in range(2)]
                    for dr_ in range(2):
                        Ux = UFx if dr_ == 0 else UBx
                        for i4 in range(4):
                            P.op("pe", lambda e, dr_=dr_, i4=i4, kb=kb, Ux=Ux, sp=sp: e.matmul(kb[dr_][0][:, i4 * 128:(i4 + 1) * 128], lhsT=Ux, rhs=sp[:, dr_, i4 * 128:(i4 + 1) * 128],
                                                                                         start=True, stop=True), [cst_tok, spt], [kb[dr_][1]])
                    EK, EKt = EK_ring.next()
                    for dr_ in range(2):
                        P.op("act", lambda e, EK=EK, dr_=dr_, kb=kb: e.activation(out=EK[:, dr_, :], in_=kb[dr_][0][:], func=AF.Exp, scale=-1.0 / 16.0), [kb[dr_][1]], [EKt])
                fb = [banks.next() for _ in range(2)]
                for dr_ in range(2):
                    U = UF if dr_ == 0 else UB
                    for i4 in range(4):
                        P.op("pe", lambda e, dr_=dr_, i4=i4, fb=fb, U=U, sp=sp: e.matmul(fb[dr_][0][:, i4 * 128:(i4 + 1) * 128], lhsT=sp[:, dr_, i4 * 128:(i4 + 1) * 128], rhs=U,
                                                                                   start=True, stop=True), [cst_tok, spt], [fb[dr_][1]])
                E, Et = E_ring.next()
                for dr_ in range(2):
                    P.op("act", lambda e, E=E, dr_=dr_, fb=fb: e.activation(out=E[:, 2 * dr_, :], in_=fb[dr_][0][:], func=AF.Exp, scale=-1.0 / 16.0), [fb[dr_][1]], [Et])
                    if full:
                        P.op("act", lambda e, E=E, dr_=dr_, fb=fb: e.activation(out=E[:, 2 * dr_ + 1, :], in_=fb[dr_][0][:], func=AF.Exp, scale=1.0 / 16.0), [fb[dr_][1]], [Et])
                P.op("dve", lambda e, E=E, tt=tt: e.tensor_copy(out=dec[0][:, tt * 8:(tt + 1) * 8], in_=E[:, 0, 63:512:64]), [Et], [dec_tok[0][tt]])
                P.op("dve", lambda e, E=E, tt=tt: e.tensor_copy(out=dec[1][:, tt * 8:(tt + 1) * 8], in_=E[:, 2, 0:512:64]), [Et], [dec_tok[1][tt]])
                if not full:
                    kt_b = banks.next()
                    for i4 in range(4):
                        ti = tt * 4 + i4
                        for c in range(NCH):
                            P.op("pe", lambda e, kt_b=kt_b, i4=i4, ti=ti, c=c, wkb=wkb: e.matmul(kt_b[0][:, i4 * 128:(i4 + 1) * 128], lhsT=hT[:, c, ti * 128:(ti + 1) * 128], rhs=wkb[:, c, :],
                                                                                    start=(c == 0), stop=(c == NCH - 1)), [htok[tt], wkb_tok], [kt_b[1]])
                    for dr_ in range(2):
                        P.op("dve", lambda e, dr_=dr_, kt_b=kt_b, EK=EK, tt=tt: e.tensor_tensor(out=kdec[dr_][:, tt * 4:(tt + 1) * 4, :].rearrange("p a b -> p (a b)"),
                                                                                             in0=kt_b[0][:], in1=EK[:, dr_, :], op=ALU.mult), [kt_b[1], EKt], [kdec_tok[dr_][tt]])
                    for i2 in range(2):
                        vb = banks.next()
                        for i1 in range(2):
                            ti = tt * 4 + i2 * 2 + i1
                            for c in range(NCH):
                                P.op("pe", lambda e, vb=vb, i1=i1, ti=ti, c=c, wvb=wvb: e.matmul(vb[0][:, i1 * 256:(i1 + 1) * 256], lhsT=hT[:, c, ti * 128:(ti + 1) * 128], rhs=wvb[:, c, :],
                                                                                     start=(c == 0), stop=(c == NCH - 1)), [htok[tt], wvb_tok], [vb[1]])
                        t0_ = tt * 4 + i2 * 2
                        P.op("act", lambda e, vb=vb, t0_=t0_: e.copy(out=v_sb[:, t0_:t0_ + 2, :].rearrange("p a b -> p (a b)"), in_=vb[0][:]), [vb[1]], [v_tok[tt]])
                    for dr_ in range(2):
                        P.dma("sp", dr["kv_s"][h][:, dr_ * 2048 + tt * 512:dr_ * 2048 + (tt + 1) * 512], kdec[dr_][:, tt * 4:(tt + 1) * 4, :].rearrange("p a b -> p (a b)"),
                              reads=[kdec_tok[dr_][tt]], writes=[dr["kv_s_dtok"][h]])
                    P.dma("sp", dr["kv_s"][h][:, 4096 + tt * 1024:4096 + (tt + 1) * 1024], v_sb[:, tt * 4:(tt + 1) * 4, :].rearrange("p a b -> p (a b)"),
                          reads=[v_tok[tt]], writes=[dr["kv_s_dtok"][h]])
                if full:
                    qb_ = banks.next()
                    kb2 = banks.next()
                    for c in range(NCH):
                        P.op("pe", lambda e, qb_=qb_, c=c, sl=sl, wqb=wqb: e.matmul(qb_[0][:], lhsT=wqb[:, c, :], rhs=hT[:, c, sl], start=(c == 0), stop=(c == NCH - 1)), [wqb_tok, htok[tt]], [qb_[1]])
                    for c in range(NCH):
                        P.op("pe", lambda e, kb2=kb2, c=c, sl=sl, wkb=wkb: e.matmul(kb2[0][:], lhsT=wkb[:, c, :], rhs=hT[:, c, sl], start=(c == 0), stop=(c == NCH - 1)), [wkb_tok, htok[tt]], [kb2[1]])
                    for dr_ in range(2):
                        P.op("dve", lambda e, dr_=dr_, qb_=qb_, E=E, sl=sl: e.scalar_tensor_tensor(out=qdec[dr_][:, sl], in0=qb_[0][:], scalar=QS, in1=E[:, 2 * dr_, :],
                                                                                              op0=ALU.mult, op1=ALU.mult), [qb_[1], Et], [qk_tok[dr_][tt]])
                        P.op("dve", lambda e, dr_=dr_, kb2=kb2, E=E, sl=sl: e.tensor_tensor(out=kinv[dr_][:, sl], in0=kb2[0][:], in1=E[:, 2 * dr_ + 1, :], op=ALU.mult),
                             [kb2[1], Et], [qk_tok[dr_][tt]])
            load_head(h + 1)
            for dr_ in range(2):
                if full:
                    P.op("act", lambda e, dr_=dr_, h=h: e.activation(out=S[dr_][:], in_=sin_sb[:, dr_ * 4 + h, :], func=AF.Identity, scale=msk_sb[:, 1 - dr_:2 - dr_]),
                         [sin_tok, msk_tok], [S_tok[dr_]])
                else:
                    P.op("dve", lambda e, dr_=dr_: e.memset(S[dr_][:], 0.0), [], [S_tok[dr_]])
            orders = [list(range(32)), list(range(31, -1, -1))]
            for step in range(32):
                for dr_ in range(2):
                    n = orders[dr_][step]
                    ti, half = n // 2, n % 2
                    tt = ti // 4
                    rows = slice(half * 64, (half + 1) * 64)
                    if full:
                        P.op("act", lambda e, dr_=dr_, n=n: e.copy(out=sst[dr_][:, n, :], in_=S[dr_][:]), [S_tok[dr_]], [sst_tok[dr_][n]])
                    if full and step == 31:
                        continue
                    bk, bkt = banks.next()
                    P.op("pe", lambda e, bk=bk, dr_=dr_, ti=ti, rows=rows: e.matmul(bk[:, 0:256], lhsT=kdec[dr_][rows, ti, :], rhs=v_sb[rows, ti, :], start=True, stop=True),
                         [kdec_tok[dr_][tt], v_tok[tt]], [bkt])
                    P.op("dve", lambda e, bk=bk, dr_=dr_, n=n: e.scalar_tensor_tensor(out=S[dr_][:], in0=S[dr_][:], scalar=dec[dr_][:, n:n + 1], in1=bk[:, 0:256],
                                                                                    op0=ALU.mult, op1=ALU.add), [S_tok[dr_], dec_tok[dr_][tt], bkt], [S_tok[dr_]])
            if not full:
                for dr_ in range(2):
                    P.op("dve", lambda e, dr_=dr_, h=h: e.tensor_copy(out=sout_sb[:, dr_ * 4 + h, :], in_=S[dr_][:]), [S_tok[dr_]], [sout_tok])
            if not full:
                continue
            def att_stage(tt, i4):
                ti = tt * 4 + i4
                tsl = slice(ti * 128, (ti + 1) * 128)
                ab, abt = banks.next()
                for dr_ in range(2):
                    P.op("pe", lambda e, ab=ab, dr_=dr_, tsl=tsl: e.matmul(ab[:, dr_ * 128:(dr_ + 1) * 128], lhsT=kinv[dr_][:, tsl], rhs=qdec[dr_][:, tsl], start=True, stop=True),
                         [qk_tok[dr_][tt]], [abt])
                at, att_t = att_ring.next()
                P.op("dve", lambda e, at=at, ab=ab: e.tensor_tensor(out=at[:, 0:128], in0=ab[:, 0:128], in1=MF, op=ALU.mult), [abt, cst_tok], [att_t])
                P.op("dve", lambda e, at=at, ab=ab: e.tensor_tensor(out=at[:, 128:256], in0=ab[:, 128:256], in1=MB, op=ALU.mult), [abt, cst_tok], [att_t])
                return at, att_t

            oitems = [(tt, i4) for tt in range(4) for i4 in range(4)]
            att_next = att_stage(*oitems[0])
            ob = None
            for k_, (tt, i4) in enumerate(oitems):
                sl = slice(tt * 512, (tt + 1) * 512)
                at, att_t = att_next
                if k_ + 1 < len(oitems):
                    att_next = att_stage(*oitems[k_ + 1])
                if i4 == 0:
                    ob = [banks.next() for _ in range(2)]
                ti = tt * 4 + i4
                for j in range(2):
                    osl = slice(i4 * 128, (i4 + 1) * 128)
                    vsl = slice(j * 128, (j + 1) * 128)
                    P.op("pe", lambda e, ob=ob, j=j, osl=osl, vsl=vsl, ti=ti, at=at: e.matmul(ob[j][0][:, osl], lhsT=v_sb[:, ti, vsl], rhs=at[:, 0:128], start=True, stop=False),
                         [v_tok[tt], att_t], [ob[j][1]])
                    P.op("pe", lambda e, ob=ob, j=j, osl=osl, vsl=vsl, ti=ti, at=at: e.matmul(ob[j][0][:, osl], lhsT=v_sb[:, ti, vsl], rhs=at[:, 128:256], start=False, stop=False),
                         [v_tok[tt], att_t], [ob[j][1]])
                    for dr_ in range(2):
                        for half in range(2):
                            n = 2 * ti + half
                            c0 = i4 * 128 + half * 64
                            q0 = ti * 128 + half * 64
                            lastmm = (dr_ == 1 and half == 1)
                            P.op("pe", lambda e, ob=ob, j=j, c0=c0, q0=q0, dr_=dr_, n=n, vsl=vsl, lastmm=lastmm: e.matmul(
                                ob[j][0][:, c0:c0 + 64], lhsT=sst[dr_][:, n, vsl], rhs=qdec[dr_][:, q0:q0 + 64], start=False, stop=lastmm),
                                [sst_tok[dr_][n], qk_tok[dr_][tt]], [ob[j][1]])
                if i4 != 3:
                    continue
                sg, sgt = sg_ring.next()
                for j in range(2):
                    gb = banks.next()
                    for c in range(NCH):
                        P.op("pe", lambda e, gb=gb, c=c, j=j, sl=sl, wgb=wgb: e.matmul(gb[0][:], lhsT=wgb[:, c, j * 128:(j + 1) * 128], rhs=hT[:, c, sl], start=(c == 0), stop=(c == NCH - 1)),
                             [wgb_tok, htok[tt]], [gb[1]])
                    P.op("act", lambda e, sg=sg, j=j, gb=gb: e.activation(out=sg[:, j, :], in_=gb[0][:], func=AF.Silu), [gb[1]], [sgt])
                oh, oht = o_ring.next()
                for j in range(2):
                    P.op("act", lambda e, oh=oh, j=j, ob=ob: e.copy(out=oh[:, j, :], in_=ob[j][0][:]), [ob[j][1]], [oht])
                sqo, sqot = sqo_ring.next()
                P.op("act", lambda e, sqo=sqo, oh=oh: e.activation(out=sqo[:], in_=oh[:], func=AF.Square), [oht], [sqot])
                mb, mbt = banks.next()
                for j in range(2):
                    P.op("pe", lambda e, mb=mb, sqo=sqo, j=j: e.matmul(mb[:], lhsT=onesV[:], rhs=sqo[:, j, :], start=(j == 0), stop=(j == 1)), [onesV_tok, sqot], [mbt])
                rso, rsot = rso_ring.next()
                P.op("act", lambda e, rso=rso, mb=mb: e.activation(out=rso[:], in_=mb[:], func=AF.Ln, bias=EPS, scale=1.0), [mbt], [rsot])
                P.op("act", lambda e, rso=rso: e.activation(out=rso[:], in_=rso[:], func=AF.Exp, scale=-0.5), [rsot], [rsot])
                on, ont = oh, oht
                for j in range(2):
                    P.op("dve", lambda e, on=on, oh=oh, j=j, rso=rso: e.scalar_tensor_tensor(out=on[:, j, :], in0=oh[:, j, :], scalar=gn_sb[:, j:j + 1], in1=rso[:],
                                                                                          op0=ALU.mult, op1=ALU.mult), [oht, gn_tok, rsot], [ont])
                of, oft = of_ring.next()
                P.op("pool", lambda e, of=of, on=on, sg=sg: e.tensor_tensor(out=of[:], in0=on[:], in1=sg[:], op=ALU.mult), [ont, sgt], [oft])
                P.dma("sp", dr["oT"][h * 256:(h + 1) * 256, sl].rearrange("(j p) t -> p j t", p=128), of[:], reads=[oft], writes=[dr["oT_dtok"]])
        if not full:
            P.dma("sp", dr["s_out"], sout_sb[:].rearrange("p a b -> p (a b)"), reads=[sout_tok], writes=[dr["s_out_dtok"]])


def oproj_body(P, banks, x_sb, xtok, dr):
    if True:
        oT = P.sbuf("oT", [128, NCH, TC], BF16); o_tok = [Tok(f"o{t}") for t in range(4)]
        ov = dr["oT"].rearrange("(c p) t -> p c t", p=128)
        ws_ring = Ring(P, "ws", [128, 1024], F32, 2)
        wob = P.sbuf("wob", [128, NCH, D], BF16); wob_tok = [Tok(f"wob{c}") for c in range(NCH)]
        P.dma("sp", oT[:, :, 0:512], ov[:, :, 0:512], reads=[dr["oT_dtok"]], writes=[o_tok[0]])
        for c in range(NCH):
            load_cast(P, dr["wo"][c], wob[:, c, :], wob_tok[c], ws_ring, eng="dve")
        for tt in range(1, 4):
            sl = slice(tt * 512, (tt + 1) * 512)
            P.dma("sp", oT[:, :, sl], ov[:, :, sl], reads=[dr["oT_dtok"]], writes=[o_tok[tt]])
        if dr.get("x_spill") is not None:
            xv = dr["x_spill"].rearrange("(c p) t -> p c t", p=128)
            for tt in range(4):
                sl = slice(tt * 512, (tt + 1) * 512)
                P.dma("sp", x_sb[:, :, sl], xv[:, :, sl], reads=[dr["x_spill_dtok"]], writes=[xtok[c][tt] for c in range(NCH)])
        for tt in range(4):
            for d in range(NCH):
                sl = slice(tt * 512, (tt + 1) * 512)
                bk, bkt = banks.next()
                for c in range(NCH):
                    P.op("pe", lambda e, bk=bk, c=c, d=d, sl=sl: e.matmul(bk[:], lhsT=wob[:, c, d * 128:(d + 1) * 128], rhs=oT[:, c, sl], start=(c == 0), stop=(c == NCH - 1)),
                         [wob_tok[c], o_tok[tt]], [bkt])
                P.op("dve", lambda e, bk=bk, d=d, sl=sl: e.tensor_tensor(out=x_sb[:, d, sl], in0=x_sb[:, d, sl], in1=bk[:], op=ALU.add), [bkt, xtok[d][tt]], [xtok[d][tt]])


def gla_consts():
    s = np.arange(128)[:, None]
    t = np.arange(128)[None, :]
    same = (s // 64) == (t // 64)
    g = np.float32(1.0)
    UF = np.where(same & (s <= t), g, 0).astype(np.float32)
    UB = np.where(same & (s >= t), g, 0).astype(np.float32)
    UFx = np.where(same & (s > t), g, 0).astype(np.float32)
    UBx = np.where(same & (s < t), g, 0).astype(np.float32)
    MF = np.where(same & (s <= t), 1, 0).astype(np.float32)
    MB = np.where(same & (s > t), 1, 0).astype(np.float32)
    return np.ascontiguousarray(np.concatenate([UF, UB, UFx, UBx, MF, MB], axis=1))


def pcf(w):
    n = w.shape[1]
    return np.ascontiguousarray(w.reshape(NCH, 128, n).transpose(1, 0, 2).reshape(128, NCH * n))


NL_DEFAULT = 4


def build_fused(nl=NL_DEFAULT):
    nc = bass.Bass("TRN2", target_bir_lowering=False)
    di = lambda n, s, dt=F32: nc.dram_tensor(n, list(s), dt, kind="ExternalInput").ap()
    dint = lambda n, s, dt=F32: nc.dram_tensor(n, list(s), dt, kind="Internal").ap()
    xT = di("xT", [D, TC])
    msk = di("msk", [128, 2])
    cs = dint("cs_scr", [128, 2 * TC])
    rot = di("rot", [128, 128])
    consts = di("consts", [128, 6 * 128])
    yT = nc.dram_tensor("yT", [D, TC], F32, kind="ExternalOutput").ap()
    L = []
    for i in range(nl):
        d = {"nwm": di(f"nwm{i}", [128, NCH]), "nwf": di(f"nwf{i}", [128, NCH]),
             "wup": di(f"wup{i}", [NFT, 128, 1024]), "cw": di(f"cw{i}", [128, NFT * 4]), "wdn": di(f"wdn{i}", [NPAIR, 128, 1024]),
             "wo": di(f"wo{i}", [NCH, 128, D]),
             "hb": dint(f"hb{i}", [128, 16]), "hg": dint(f"hg{i}", [256, 16])}
        if i % 2 == 0:
            d.update({"wq": di(f"wq{i}", [4, 128, 1024]), "wk": di(f"wk{i}", [4, 128, 1024]), "wv": di(f"wv{i}", [4, 128, 2048]),
                      "wg": di(f"wg{i}", [4, 128, 2048]), "wr": di(f"wr{i}", [128, 256]), "wgate": di(f"wgate{i}", [33, 1024]),
                      "gn": di(f"gn{i}", [128, 2]),
                      "st_b": dint(f"stb{i}", [128, 2048]), "st_g": dint(f"stg{i}", [256, 2048]),
                      "oT": dint(f"oTs{i}", [D, TC], BF16), "x_spill": dint(f"xsp{i}", [D, TC]),
                      "kv_s": dint(f"kvs{i}", [4, 128, 8192], BF16)})
        else:
            d.update({"wqk": di(f"wqk{i}", [10, 128, 1024]), "wv": di(f"wva{i}", [128, NCH * 256]), "gains": di(f"gains{i}", [128, 2]),
                      "q_s": dint(f"qs{i}", [8, 128, TC], BF16),
                      "kv_b": dint(f"kvb{i}", [512, TC], BF16), "kv_g": dint(f"kvg{i}", [1024, TC], BF16),
                      "oT": dint(f"oTs{i}", [D, TC], BF16)})
        L.append(d)
    with contextlib.ExitStack() as es:
        P = Prog(nc, es)
        bank_aps = [P.psum(f"bk{i}", [128, 512]) for i in range(8)]
        bank_toks = [Tok(f"bk{i}") for i in range(8)]
        banks8 = Banks(P, 8, aps=bank_aps, toks=bank_toks)
        banks6 = Banks(P, 6, aps=bank_aps[0:6], toks=bank_toks[0:6])
        hbanks = Banks(P, 2, aps=bank_aps[6:8], toks=bank_toks[6:8])
        sbanks = Banks(P, 3, aps=bank_aps[0:3], toks=bank_toks[0:3])
        obanks = Banks(P, 2, aps=bank_aps[3:5], toks=bank_toks[3:5])
        dbanks = Banks(P, 2, aps=bank_aps[5:7], toks=bank_toks[5:7])
        x_sb = P.sbuf("x", [128, NCH, TC], F32)
        xtok = [[Tok(f"x{c}_{t}") for t in range(4)] for c in range(NCH)]
        msk_sb = P.sbuf("msk", [128, 2], F32); msk_tok = Tok("msk")
        P.dma("sp", msk_sb[:], msk[:, :], writes=[msk_tok])
        load_x(P, xT, x_sb, xtok)
        cs_dtok = Tok("cs_d")
        with P.phase():
            cs_gen = P.sbuf("csgen", [128, 2 * TC], F32); cs_gen_tok = Tok("csgen")
            rope_gen(P, cs_gen, cs_gen_tok, msk_sb, msk_tok)
            P.dma("sp", cs[:, :], cs_gen[:], reads=[cs_gen_tok], writes=[cs_dtok])
        for i in range(nl):
            d = L[i]
            wo_list = [d["wo"][c] for c in range(NCH)]
            if i % 2 == 0:
                stb_tok, stg_tok, oT_tok, xsp_tok = Tok(f"stb{i}"), Tok(f"stg{i}"), Tok(f"oTd{i}"), Tok(f"xsp{i}")
                base = {"nw": d["nwm"][:, :], "wq": [d["wq"][h] for h in range(4)], "wk": [d["wk"][h] for h in range(4)],
                        "wv": [d["wv"][h] for h in range(4)], "wg": [d["wg"][h] for h in range(4)], "wr": d["wr"][:, :],
                        "wgate": d["wgate"][:, :], "gn": d["gn"][:, :], "consts": consts[:, :],
                        "kv_s": [d["kv_s"][h] for h in range(4)], "kv_s_dtok": [Tok(f"kvs{i}_{h}") for h in range(4)]}
                with P.phase():
                    gla_body(P, banks8, False, x_sb, xtok, msk_sb, msk_tok, dict(base, s_out=d["st_b"][:, :], s_out_dtok=stb_tok))
                P.cc(d["st_g"][:, :], d["st_b"][:, :], reads=[stb_tok], writes=[stg_tok])
                with P.phase():
                    gla_body(P, banks8, True, x_sb, xtok, msk_sb, msk_tok,
                             dict(base, s_g=d["st_g"], s_g_dtok=stg_tok, oT=d["oT"], oT_dtok=oT_tok, x_spill=d["x_spill"], x_spill_dtok=xsp_tok))
                with P.phase():
                    oproj_body(P, banks8, x_sb, xtok, {"oT": d["oT"], "oT_dtok": oT_tok, "wo": wo_list, "x_spill": d["x_spill"], "x_spill_dtok": xsp_tok})
            else:
                q_tok, kvb_tok, kvg_tok = Tok(f"qd{i}"), Tok(f"kvb{i}"), Tok(f"kvg{i}")
                kvb, kvg = d["kv_b"], d["kv_g"]
                dr1 = {"nw": d["nwm"][:, :], "wqk": [d["wqk"][f] for f in range(10)], "wv": d["wv"][:, :], "gains": d["gains"][:, :],
                       "cs": cs[:, :], "cs_dtok": cs_dtok, "rot": rot[:, :],
                       "q_out": [d["q_s"][h] for h in range(8)], "k_out": [kvb[h * 128:(h + 1) * 128, :] for h in range(2)],
                       "v_out": kvb[256:512, :].rearrange("r (a f) -> (r a) f", f=256),
                       "q_dtok": q_tok, "kv_dtok": kvb_tok}
                dr1["after_kv"] = lambda kvg=kvg, kvb=kvb, kvb_tok=kvb_tok, kvg_tok=kvg_tok: P.cc(kvg[:, :], kvb[:, :], reads=[kvb_tok], writes=[kvg_tok])
                with P.phase():
                    attn1_body(P, banks8, x_sb, xtok, dr1)
                dr2 = {"q": [d["q_s"][h] for h in range(8)],
                       "k": [[kvg[r * 512 + h * 128:r * 512 + (h + 1) * 128, :] for r in range(2)] for h in range(2)],
                       "v": [kvg[r * 512 + 256:r * 512 + 512, :].rearrange("r (a f) -> (r a) f", f=256) for r in range(2)],
                       "wo": wo_list, "q_dtok": q_tok, "kvg_dtok": kvg_tok}
                with P.phase():
                    attn2_body(P, sbanks, obanks, dbanks, x_sb, xtok, dr2)
            hb_tok, hg_tok = Tok(f"hb{i}"), Tok(f"hg{i}")
            with P.phase():
                hb_sb = P.sbuf("hb", [128, 16], F32); hbs_tok = Tok("hbs")
                hg_sb = P.sbuf("hg", [128, 2, 16], F32); hgs_tok = Tok("hgs")
                xh_sb = P.sbuf("xh", [128, NCH, 2], F32); xh_tok = Tok("xh")
                P.op("dve", lambda e, hb_sb=hb_sb: e.tensor_copy(out=hb_sb[:, 0:16:2], in_=x_sb[:, :, 0]), [xtok[c][0] for c in range(NCH)], [hbs_tok])
                P.op("dve", lambda e, hb_sb=hb_sb: e.tensor_copy(out=hb_sb[:, 1:16:2], in_=x_sb[:, :, TC - 1]), [xtok[c][3] for c in range(NCH)], [hbs_tok])
                P.dma("sp", d["hb"][:, :], hb_sb[:], reads=[hbs_tok], writes=[hb_tok])
                P.cc(d["hg"][:, :], d["hb"][:, :], reads=[hb_tok], writes=[hg_tok])
                P.dma("sp", hg_sb[:], d["hg"].rearrange("(r p) e -> p r e", p=128), reads=[hg_tok], writes=[hgs_tok])
                P.op("act", lambda e, xh_sb=xh_sb, hg_sb=hg_sb: e.activation(out=xh_sb[:, :, 0], in_=hg_sb[:, 0, 1:16:2], func=AF.Identity, scale=msk_sb[:, 1:2]),
                     [hgs_tok, msk_tok], [xh_tok])
                P.op("act", lambda e, xh_sb=xh_sb, hg_sb=hg_sb: e.activation(out=xh_sb[:, :, 1], in_=hg_sb[:, 1, 0:16:2], func=AF.Identity, scale=msk_sb[:, 0:1]),
                     [hgs_tok, msk_tok], [xh_tok])
                drf = {"nw": d["nwf"][:, :], "cw": d["cw"][:, :], "wup": [d["wup"][f] for f in range(NFT)], "wdn": [d["wdn"][j] for j in range(NPAIR)]}
                ffn_body(P, banks6, hbanks, x_sb, xtok, xh_sb, xh_tok, drf)
        store_x(P, yT, x_sb, xtok)
        P.final_wait()
        P.emit()
        print("fused program:", P.ninstr, "waits", P.nwaits, "sems", P.nsem)
    return nc


def fused_inputs(nl, x, norm_mix, norm_ffn, gla_w_in, gla_w_gate_up_f, gla_b_gate_f, gla_w_gate_up_b, gla_b_gate_b,
                 gla_norm, gla_w_out, attn_w_qkv, attn_q_norm, attn_k_norm, attn_w_out,
                 ffn_w_up, ffn_w_conv, ffn_b_conv, ffn_w_down):
    f = lambda a: np.asarray(a, dtype=np.float32)
    x = f(x)
    shared = {"rot": rot_matrix(), "consts": gla_consts()}
    for i in range(nl):
        j = i // 2
        wup, cw, wdn = ffn_weights_layout(f(ffn_w_up[i]), f(ffn_w_conv[i]), f(ffn_b_conv[i]), f(ffn_w_down[i]))
        shared.update({f"nwm{i}": fm(f(norm_mix[i])), f"nwf{i}": fm(f(norm_ffn[i])), f"wup{i}": wup, f"cw{i}": cw, f"wdn{i}": wdn})
        if i % 2 == 0:
            w_in = f(gla_w_in[j])
            wgate = np.zeros((33, 1024), np.float32)
            wgate[0:16, 0:512] = f(gla_w_gate_up_f[j]); wgate[32, 0:512] = f(gla_b_gate_f[j])
            wgate[16:32, 512:1024] = f(gla_w_gate_up_b[j]); wgate[32, 512:1024] = f(gla_b_gate_b[j])
            shared.update({
                f"wq{i}": np.stack([pcf(w_in[:, h * 128:(h + 1) * 128]) for h in range(4)]),
                f"wk{i}": np.stack([pcf(w_in[:, 512 + h * 128:512 + (h + 1) * 128]) for h in range(4)]),
                f"wv{i}": np.stack([pcf(w_in[:, 1024 + h * 256:1024 + (h + 1) * 256]) for h in range(4)]),
                f"wg{i}": np.stack([pcf(w_in[:, 2048 + h * 256:2048 + (h + 1) * 256]) for h in range(4)]),
                f"wr{i}": pcf(w_in[:, 3072:3104]), f"wgate{i}": wgate,
                f"gn{i}": np.ascontiguousarray(f(gla_norm[j]).reshape(2, 128).T),
                f"wo{i}": np.ascontiguousarray(f(gla_w_out[j]).reshape(NCH, 128, D))})
        else:
            w_qkv = f(attn_w_qkv[j])
            shared.update({
                f"wqk{i}": tile_w(w_qkv[:, :1280]),
                f"wva{i}": np.ascontiguousarray(w_qkv[:, 1280:].reshape(NCH, 128, 256).transpose(1, 0, 2).reshape(128, NCH * 256)),
                f"gains{i}": np.ascontiguousarray(np.stack([f(attn_q_norm[j]), f(attn_k_norm[j])], 1)),
                f"wo{i}": np.ascontiguousarray(f(attn_w_out[j]).reshape(NCH, 128, D))})
    in_maps = []
    for c in range(NCORES):
        hs = slice((c % 2) * TC, (c % 2 + 1) * TC)
        m = np.zeros((128, 2), np.float32)
        m[:, c % 2] = 1.0
        dct = dict(shared)
        dct["xT"] = np.ascontiguousarray(x[c // 2, hs].T)
        dct["msk"] = m
        in_maps.append(dct)
    return in_maps


def kernel(**inputs):
    nc = get_prog("fused", build_fused)
    in_maps = fused_inputs(NL_DEFAULT, **inputs)
    res = run_bass_kernel_spmd(nc, in_maps, core_ids=list(range(NCORES))).results
    out = np.empty((4, SEQ, D), np.float32)
    for c in range(NCORES):
        out[c // 2, (c % 2) * TC:(c % 2 + 1) * TC] = np.asarray(res[c]["yT"]).T
    return out
```
